# Optimizing a Trainium2 kernel written in Bass

```python
import jax, jax.numpy as jnp
from jax import lax
import numpy as np

D_MODEL = 1024
BATCH = 8
SEQ = 2048
DEPTH = 2

CHUNK = 64
EPS = 1e-6
LRU_WIDTH = D_MODEL
LRU_BLOCKS = 16
LRU_BLOCK_DIM = LRU_WIDTH // LRU_BLOCKS
LRU_CONV = 4
LRU_C = 8.0
RET_QK_DIM = 256
RET_V_DIM = 512
RET_HEADS = D_MODEL // RET_QK_DIM
RET_QK = RET_HEADS * RET_QK_DIM
RET_V = RET_HEADS * RET_V_DIM
ROPE_BASE = 10000.0
SC_WIDTH = D_MODEL
SC_CONV = 3
D_FF = 2816
SPLIT_SIZES = (LRU_WIDTH, LRU_WIDTH, RET_QK, RET_QK, RET_V, RET_V,
               SC_WIDTH, SC_WIDTH, SC_WIDTH, 3 * D_MODEL)
MIX_IN = sum(SPLIT_SIZES)

kernel_name = "hybrid_lru_retention_shortconv_macaron"


def rmsnorm(x, g):
    xf = x.astype(jnp.float32)
    xf = xf * lax.rsqrt(jnp.mean(xf * xf, axis=-1, keepdims=True) + EPS)
    return xf.astype(x.dtype) * g


def swiglu(u, w_in, w_out):
    gate, up = jnp.split(u @ w_in, 2, axis=-1)
    return (jax.nn.silu(gate) * up) @ w_out


def causal_depthwise_conv(x, w):
    K, C = w.shape
    return lax.conv_general_dilated(
        x, w[:, None, :].astype(x.dtype), window_strides=(1,), padding=[(K - 1, 0)],
        dimension_numbers=("NWC", "WIO", "NWC"), feature_group_count=C)


def _lin_rec_combine(e1, e2):
    a1, b1 = e1
    a2, b2 = e2
    return a1 * a2, a2 * b1 + b2


def rg_lru(x, w_a, b_a, w_x, b_x, lam):
    Bsz, S, W = x.shape
    xf = x.astype(jnp.float32)
    xb = xf.reshape(Bsz, S, LRU_BLOCKS, LRU_BLOCK_DIM)
    r = jax.nn.sigmoid(jnp.einsum("bsgi,gij->bsgj", xb, w_a.astype(jnp.float32)).reshape(Bsz, S, W)
                       + b_a.astype(jnp.float32))
    i = jax.nn.sigmoid(jnp.einsum("bsgi,gij->bsgj", xb, w_x.astype(jnp.float32)).reshape(Bsz, S, W)
                       + b_x.astype(jnp.float32))
    log_a = -LRU_C * r * jax.nn.softplus(-lam.astype(jnp.float32))
    a = jnp.exp(log_a)
    u = jnp.sqrt(-jnp.expm1(2.0 * log_a)) * (i * xf)
    _, h = lax.associative_scan(_lin_rec_combine, (a, u), axis=1)
    return h.astype(x.dtype)


def rope(t, positions):
    half = t.shape[-1] // 2
    inv_freq = jnp.power(ROPE_BASE, -jnp.arange(half, dtype=jnp.float32) / half)
    ang = positions.astype(jnp.float32)[:, None] * inv_freq[None, :]
    cos = jnp.cos(ang)[None, :, None, :]
    sin = jnp.sin(ang)[None, :, None, :]
    t1, t2 = t[..., :half], t[..., half:]
    return jnp.concatenate([t1 * cos - t2 * sin, t1 * sin + t2 * cos], axis=-1)


def retention(q, k, v, positions):
    Bsz, S, H, dk = q.shape
    dv = v.shape[-1]
    N = S // CHUNK
    q = rope(q.astype(jnp.float32), positions)
    k = rope(k.astype(jnp.float32), positions) * (dk ** -0.5)
    v = v.astype(jnp.float32)
    log_g = jnp.log1p(-jnp.power(2.0, -5.0 - jnp.arange(H, dtype=jnp.float32)))
    idx = jnp.arange(CHUNK, dtype=jnp.float32)
    intra_decay = jnp.exp(log_g[:, None, None] * jnp.abs(idx[:, None] - idx[None, :]))
    q_decay = jnp.exp(log_g[:, None] * (idx[None, :] + 1.0))
    k_decay = jnp.exp(log_g[:, None] * (CHUNK - 1.0 - idx[None, :]))
    chunk_decay = jnp.exp(log_g * CHUNK)

    def to_chunks(t):
        return t.reshape(Bsz, N, CHUNK, H, t.shape[-1]).transpose(1, 0, 3, 2, 4)

    qc, kc, vc = to_chunks(q), to_chunks(k), to_chunks(v)
    scores = jnp.einsum("nbhid,nbhjd->nbhij", qc, kc) * intra_decay
    intra = jnp.einsum("nbhij,nbhje->nbhie", scores, vc)

    def step(state, inp):
        qn, kn, vn = inp
        cross = jnp.einsum("bhid,bhde->bhie", qn, state) * q_decay[None, :, :, None]
        state = state * chunk_decay[None, :, None, None] + jnp.einsum(
            "bhjd,bhje->bhde", kn * k_decay[None, :, :, None], vn)
        return state, cross

    state0 = jnp.zeros((Bsz, H, dk, dv), jnp.float32)
    _, cross = lax.scan(step, state0, (qc, kc, vc))
    out = (intra + cross).transpose(1, 0, 3, 2, 4).reshape(Bsz, S, H, dv)
    out = out * lax.rsqrt(jnp.mean(out * out, axis=-1, keepdims=True) + EPS)
    return out


def hybrid_mixer(u, positions, w_in, lru_conv_w, lru_conv_b, lru_w_a, lru_b_a, lru_w_x, lru_b_x,
                 lru_lambda, w_lru_out, w_ret_out, sc_conv_w, w_sc_out, w_out):
    Bsz, S, _ = u.shape
    proj = u @ w_in
    cuts = [int(c) for c in np.cumsum(SPLIT_SIZES)[:-1]]
    xa, ya, q, k, v, g, sb, sc, sh, gates = jnp.split(proj, cuts, axis=-1)
    xa = causal_depthwise_conv(xa, lru_conv_w) + lru_conv_b
    ha = rg_lru(xa, lru_w_a, lru_b_a, lru_w_x, lru_b_x, lru_lambda)
    oa = (ha * jax.nn.gelu(ya)) @ w_lru_out
    ret = retention(q.reshape(Bsz, S, RET_HEADS, RET_QK_DIM),
                    k.reshape(Bsz, S, RET_HEADS, RET_QK_DIM),
                    v.reshape(Bsz, S, RET_HEADS, RET_V_DIM), positions)
    ob = (jax.nn.silu(g) * ret.reshape(Bsz, S, RET_V).astype(u.dtype)) @ w_ret_out
    oc = (sb * causal_depthwise_conv(sc * sh, sc_conv_w)) @ w_sc_out
    ga, gb, gc = jnp.split(jax.nn.sigmoid(gates), 3, axis=-1)
    return (ga * oa + gb * ob + gc * oc) @ w_out


def setup_inputs(seed: int = 0) -> dict:
    key = jax.random.key(seed)
    ks = jax.random.split(key, 32)
    L = DEPTH

    def nrm(k, shape, scale):
        return jax.random.normal(k, shape, jnp.float32) * scale

    def gain(k):
        return 1.0 + 0.02 * jax.random.normal(k, (L, D_MODEL), jnp.float32)

    a_target = jax.random.uniform(ks[2], (L, LRU_WIDTH), jnp.float32, minval=0.9, maxval=0.999)
    s = a_target ** (1.0 / LRU_C)
    lru_lambda = jnp.log(s) - jnp.log1p(-s)
    offset = CHUNK * jax.random.randint(ks[3], (), 0, 64, dtype=jnp.int32)
    return {
        "x": nrm(ks[0], (BATCH, SEQ, D_MODEL), 1.0),
        "positions": offset + jnp.arange(SEQ, dtype=jnp.int32),
        "ffn1_pre_g": gain(ks[4]),
        "ffn1_w_in": nrm(ks[5], (L, D_MODEL, 2 * D_FF), D_MODEL ** -0.5),
        "ffn1_w_out": nrm(ks[6], (L, D_FF, D_MODEL), D_FF ** -0.5),
        "ffn1_post_g": gain(ks[7]),
        "mix_pre_g": gain(ks[8]),
        "w_mix_in": nrm(ks[9], (L, D_MODEL, MIX_IN), D_MODEL ** -0.5),
        "lru_conv_w": nrm(ks[10], (L, LRU_CONV, LRU_WIDTH), LRU_CONV ** -0.5),
        "lru_conv_b": nrm(ks[11], (L, LRU_WIDTH), 0.01),
        "lru_w_a": nrm(ks[12], (L, LRU_BLOCKS, LRU_BLOCK_DIM, LRU_BLOCK_DIM), LRU_BLOCK_DIM ** -0.5),
        "lru_b_a": nrm(ks[13], (L, LRU_WIDTH), 0.01),
        "lru_w_x": nrm(ks[14], (L, LRU_BLOCKS, LRU_BLOCK_DIM, LRU_BLOCK_DIM), LRU_BLOCK_DIM ** -0.5),
        "lru_b_x": nrm(ks[15], (L, LRU_WIDTH), 0.01),
        "lru_lambda": lru_lambda,
        "w_lru_out": nrm(ks[16], (L, LRU_WIDTH, D_MODEL), LRU_WIDTH ** -0.5),
        "w_ret_out": nrm(ks[17], (L, RET_V, D_MODEL), RET_V ** -0.5),
        "sc_conv_w": nrm(ks[18], (L, SC_CONV, SC_WIDTH), SC_CONV ** -0.5),
        "w_sc_out": nrm(ks[19], (L, SC_WIDTH, D_MODEL), SC_WIDTH ** -0.5),
        "w_mix_out": nrm(ks[20], (L, D_MODEL, D_MODEL), D_MODEL ** -0.5),
        "mix_post_g": gain(ks[21]),
        "ffn2_pre_g": gain(ks[22]),
        "ffn2_w_in": nrm(ks[23], (L, D_MODEL, 2 * D_FF), D_MODEL ** -0.5),
        "ffn2_w_out": nrm(ks[24], (L, D_FF, D_MODEL), D_FF ** -0.5),
        "ffn2_post_g": gain(ks[25]),
    }


def reference(x, positions, ffn1_pre_g, ffn1_w_in, ffn1_w_out, ffn1_post_g, mix_pre_g, w_mix_in,
              lru_conv_w, lru_conv_b, lru_w_a, lru_b_a, lru_w_x, lru_b_x, lru_lambda, w_lru_out,
              w_ret_out, sc_conv_w, w_sc_out, w_mix_out, mix_post_g, ffn2_pre_g, ffn2_w_in,
              ffn2_w_out, ffn2_post_g):
    for l in range(DEPTH):
        h = swiglu(rmsnorm(x, ffn1_pre_g[l]), ffn1_w_in[l], ffn1_w_out[l])
        x = x + 0.5 * rmsnorm(h, ffn1_post_g[l])
        h = hybrid_mixer(rmsnorm(x, mix_pre_g[l]), positions, w_mix_in[l], lru_conv_w[l],
                         lru_conv_b[l], lru_w_a[l], lru_b_a[l], lru_w_x[l], lru_b_x[l],
                         lru_lambda[l], w_lru_out[l], w_ret_out[l], sc_conv_w[l], w_sc_out[l],
                         w_mix_out[l])
        x = x + rmsnorm(h, mix_post_g[l])
        h = swiglu(rmsnorm(x, ffn2_pre_g[l]), ffn2_w_in[l], ffn2_w_out[l])
        x = x + 0.5 * rmsnorm(h, ffn2_post_g[l])
    return x
```

```python
import contextlib
import math
import numpy as np
import concourse.bass as bass
import concourse.mybir as mybir
from concourse.bass_utils import run_bass_kernel_spmd

F32 = mybir.dt.float32
F32R = mybir.dt.float32r
I32 = mybir.dt.int32
AF = mybir.ActivationFunctionType
ALU = mybir.AluOpType

D = 1024
KC = 8
SEQ = 2048
TT = 512
NT = SEQ // TT
DFF = 2816
NFF = 22
MIXIN = 14336
NL = 2
EPS = 1e-6
NSLOT = 3
NPC = 136

ENGS = ["pe", "act", "dve", "pool", "sp"]
SELF_SYNC = {"pe": False, "act": True, "dve": True, "pool": True, "sp": False}


class Cell:
    __slots__ = ("name", "w", "r")

    def __init__(self, name):
        self.name = name
        self.w = None
        self.r = {}


def cells(name, n):
    return [Cell(f"{name}{i}") for i in range(n)]


class Prog:
    def __init__(self):
        self.ops = {e: [] for e in ENGS}
        self.cnt = {}
        self.seen = {e: {} for e in ENGS}

    def op(self, eng, fn, reads=(), writes=(), key=None, inc=1):
        need = {}

        def add(tok):
            if tok is None:
                return
            k, v = tok
            if need.get(k, 0) < v:
                need[k] = v

        for c in reads:
            add(c.w)
        for c in writes:
            add(c.w)
            for k, v in c.r.items():
                add((k, v))
        waits = []
        seen = self.seen[eng]
        for k, v in need.items():
            if k == eng and not SELF_SYNC[eng]:
                continue
            if seen.get(k, 0) < v:
                seen[k] = v
                waits.append((k, v))
        k = key or eng
        self.cnt[k] = self.cnt.get(k, 0) + inc
        tok = (k, self.cnt[k])
        for c in reads:
            if c.r.get(k, 0) < tok[1]:
                c.r[k] = tok[1]
        for c in writes:
            c.w = tok
            c.r = {}
        self.ops[eng].append((waits, fn, k, inc))
        return tok

    def wait_all(self, eng, toks):
        self.ops[eng].append((list(toks), None, None, 0))

    def replay(self, eng, e, sems):
        for waits, fn, k, inc in self.ops[eng]:
            for (wk, wv) in waits:
                e.wait_ge(sems[wk], wv)
            if fn is None:
                continue
            ins = fn(e)
            ins.then_inc(sems[k], inc)


def r_(ap):
    return ap.bitcast(F32R)


def head_consts():
    H = 4
    idx = np.arange(128)
    cols = []
    inv_freq = np.power(np.float32(10000.0), -np.arange(128, dtype=np.float32) / np.float32(128)).astype(np.float32)
    cols.append(inv_freq[:, None])
    gam = [1.0 - 2.0 ** (-5.0 - h) for h in range(H)]
    jj, ii = idx[:, None].astype(np.float64), idx[None, :].astype(np.float64)
    cj, ci = idx[:, None] // 64, idx[None, :] // 64
    for h in range(H):
        g = gam[h]
        m = np.where(cj == ci, g ** np.abs(ii - jj), np.where(cj < ci, g ** (ii - jj), 0.0)) / 16.0
        cols.append(m.astype(np.float32))
    for h in range(H):
        cols.append((gam[h] ** (ii - jj) / 16.0).astype(np.float32))
    for h in range(H):
        v = (gam[h] ** (idx + 1.0)).astype(np.float32)
        cols.append(np.tile(v[None, :], (128, 1)))
    kd = np.stack([gam[h] ** (511.0 - jb * 128.0 - idx) / 16.0 for h in range(H) for jb in range(4)], 1).astype(np.float32)
    cols.append(kd)
    cols.append(np.eye(128, dtype=np.float32))
    cst = np.concatenate(cols, 1).astype(np.float32)
    gp = [{d: float(gam[h] ** (128.0 * d)) for d in range(5)} for h in range(H)]
    return cst, gp


C_INV = 0
C_MASK = 1
C_BASE = 513
C_QD = 1025
C_KD = 1537
C_ID = 1553
NCST = 1681

PO = dict(ffn1_pre=0, ffn1_post=8, mix_pre=16, mix_post=24, ffn2_pre=32, ffn2_post=40,
          conv_w=48, conv_b=80, b_a=88, b_x=96, lam=104, sc_w=112)


def build(n_layers=NL, stop=None, dump=None):
    cst_np, G64 = head_consts()
    nc = bass.Bass("TRN2", target_bir_lowering=False)
    xT = nc.dram_tensor("xT", [D, SEQ], F32, kind="ExternalInput").ap()
    pos = nc.dram_tensor("pos", [1, SEQ], I32, kind="ExternalInput").ap()
    pp_d = nc.dram_tensor("pp", [128, NL * NPC], F32, kind="ExternalInput").ap()
    cst_d = nc.dram_tensor("cst", [128, NCST], F32, kind="ExternalInput").ap()
    wbd_d = nc.dram_tensor("wbd", [NL, 128, 2 * KC * 128], F32, kind="ExternalInput").ap()
    W = {}
    for nm, shp in [("ffn1_w_in", [NL, D, 2 * DFF]), ("ffn1_w_out", [NL, DFF, D]), ("w_mix_in", [NL, D, MIXIN]),
                    ("w_lru_out", [NL, D, D]), ("w_ret_out", [NL, 2 * D, D]), ("w_sc_out", [NL, D, D]),
                    ("w_mix_out", [NL, D, D]), ("ffn2_w_in", [NL, D, 2 * DFF]), ("ffn2_w_out", [NL, DFF, D])]:
        W[nm] = nc.dram_tensor(nm, shp, F32, kind="ExternalInput").ap()
    outT = nc.dram_tensor("outT", [D, SEQ], F32, kind="ExternalOutput").ap()
    dbg = None
    if dump:
        dbg = nc.dram_tensor("dbg", [NT, 128, 16, TT], F32, kind="ExternalOutput").ap()
    xTv = xT.rearrange("(kc p) t -> p kc t", p=128)
    oTv = outT.rearrange("(kc p) t -> p kc t", p=128)

    P = Prog()
    with contextlib.ExitStack() as st:
        def sb(name, shape, dt=F32):
            return st.enter_context(nc.sbuf_tensor(name, shape, dt))

        xres = sb("xres", [128, KC, TT]); c_x = cells("x", KC)
        ubuf = sb("ubuf", [128, KC, TT]); c_u = cells("u", KC)
        big = sb("big", [128, 16, TT]); c_big = cells("big", 16)
        Sb = sb("Sb", [128, KC, TT]); c_S = cells("S", KC)
        slots = [sb(f"wslot{i}", [128, 2048]) for i in range(NSLOT)]; c_slot = cells("slot", NSLOT)
        NTMP = 6
        tmp = [sb(f"tmp{i}", [128, TT]) for i in range(NTMP)]; c_tmp = cells("tmp", NTMP)
        tmpR = [sb(f"tmpR{i}", [128, TT]) for i in range(2)]; c_tmpR = cells("tmpR", 2)
        xraw = sb("xraw", [128, TT + 4]); c_xraw = Cell("xraw")
        rstd = sb("rstd", [128, TT]); c_rstd = Cell("rstd")
        cosT = sb("cosT", [128, TT]); sinT = sb("sinT", [128, TT]); c_cs = Cell("cs")
        posi = sb("posi", [128, TT], I32); c_posi = Cell("posi")
        ki = posi; c_ki = c_posi
        pp = sb("pp_sb", [128, NL * NPC]); c_pp = Cell("pp")
        ghalf = sb("ghalf", [128, NL, 2, KC]); nsp = sb("nsp", [128, NL, 2, KC]); c_der = Cell("der")
        cst = sb("cst_sb", [128, NCST]); c_cst = Cell("cst")
        ones = sb("ones", [128, 128]); c_ones = Cell("ones")
        halo = sb("halo", [128, NL, KC, 4]); c_halo = [cells(f"halo{l}_", KC) for l in range(NL)]
        schalo = sb("schalo", [128, NL, KC, 2]); c_schalo = [cells(f"sch{l}_", KC) for l in range(NL)]
        hst = sb("hst", [128, NL, KC]); c_hst = [cells(f"hst{l}_", KC) for l in range(NL)]
        rst = [[sb(f"rst{l}_{h}", [128, 2, TT]) for h in range(4)] for l in range(NL)]
        c_rst = [[Cell(f"rst{l}_{h}") for h in range(4)] for l in range(NL)]
        qT = sb("qT", [128, 2, TT]); c_qT = Cell("qT")
        kT = sb("kT", [128, 2, TT]); c_kT = Cell("kT")
        vtok = sb("vtok", [128, 4, TT]); c_vtok = cells("vtok", 4)
        rtok = sb("rtok", [128, 4, TT]); c_rtok = cells("rtok", 4)
        S2 = sb("S2", [128, 1280]); c_S2 = Cell("S2")
        wgb = sb("wgb", [128, 2, 2, 128]); c_wgb = Cell("wgb")
        qd = sb("qd", [128, 2, TT]); c_qd = Cell("qd")
        ssq = sb("ssq", [128, 8]); c_ssq = cells("ssq", 8)
        banks = [st.enter_context(nc.psum_tensor(f"ps{i}", [128, TT], F32)) for i in range(8)]
        c_bank = cells("bank", 8)

        state = {"bank": 0, "slot": 0, "tmp": 0, "ssq": 0, "qeo": 0}
        SL = [(slots[i][:, :], [c_slot[i]], f"w{i}") for i in range(NSLOT)]
        SL_V = (vtok[:].rearrange("p a b -> p (a b)"), list(c_vtok), "wv")
        SL_B = (big[:, 12:16, :].rearrange("p a b -> p (a b)"), list(c_big[12:16]), "wb")
        SL_B2 = (big[:, 8:12, :].rearrange("p a b -> p (a b)"), list(c_big[8:12]), "wb2")
        POOL_FFN = SL + [SL_V, SL_B]
        POOL_C = SL + [SL_V, SL_B, SL_B2]
        POOL_MIX = SL + [SL_V]
        POOL_B = SL
        state["pool"] = POOL_FFN

        HELD = set()

        def nb():
            while True:
                b = state["bank"]; state["bank"] = (b + 1) % 8
                if b not in HELD:
                    return b

        def nt():
            t = state["tmp"]; state["tmp"] = (t + 1) % NTMP
            return t

        def ntr():
            t = state["ssq"] % 2; state["tmpr"] = state.get("tmpr", 0) + 1
            return state["tmpr"] % 2

        def wload(src, nk, ncols):
            pool = state["pool"]
            state["slot"] = (state["slot"] + 1) % len(pool)
            buf, cl, key = pool[state["slot"]]
            view = buf[:, 0:nk * ncols].rearrange("p (k n) -> p k n", k=nk)
            srcv = src.rearrange("(k p) n -> p k n", p=128)
            P.op("pool", lambda e: e.dma_start(out=r_(view), in_=srcv), writes=cl, key=key, inc=16)
            return view, cl

        def mmgroup(bank, col0, ncol, terms, reads):
            def fn(e):
                ins = None
                n = len(terms)
                for i, (l, r) in enumerate(terms):
                    ins = e.matmul(banks[bank][0:l.shape[-1], col0:col0 + ncol], l, r, start=(i == 0), stop=(i == n - 1))
                return ins
            P.op("pe", fn, reads=reads, writes=[c_bank[bank]])

        def mmstream(bank, terms, wcells):
            n = len(terms)
            for i, (l_, r) in enumerate(terms):
                P.op("pe", (lambda i=i, l_=l_, r=r: lambda e: e.matmul(banks[bank][:], l_, r, start=(i == 0), stop=(i == n - 1)))(),
                     reads=wcells + [c_u[i]], writes=[c_bank[bank]])

        def act(out, in_, func, reads, writes, **kw):
            P.op("act", lambda e: e.activation(out, in_, func, **kw), reads=reads, writes=writes)

        def dve_tt(out, a, b, op, reads, writes):
            P.op("dve", lambda e: e.tensor_tensor(out, a, b, op), reads=reads, writes=writes)

        def dve_ts(out, a, s1, s2, op0, op1, reads, writes):
            if s2 is None:
                P.op("dve", lambda e: e.tensor_scalar(out, a, s1, None, op0), reads=reads, writes=writes)
            else:
                P.op("dve", lambda e: e.tensor_scalar(out, a, s1, s2, op0, op1), reads=reads, writes=writes)

        def dve_stt(out, a, s, b, op0, op1, reads, writes):
            P.op("dve", lambda e: e.scalar_tensor_tensor(out, a, s, b, op0, op1), reads=reads, writes=writes)

        def dve_copy(out, a, reads, writes):
            P.op("dve", lambda e: e.tensor_copy(out, a), reads=reads, writes=writes)

        def ppcol(l, name, kc, j=0):
            o = l * NPC + PO[name] + j * KC + kc
            return pp[:, o:o + 1]

        P.op("sp", lambda e: e.dma_start(out=pp[:], in_=pp_d[:]), writes=[c_pp], key="dpp", inc=16)
        P.op("sp", lambda e: e.dma_start(out=cst[:], in_=cst_d[:]), writes=[c_cst], key="dcst", inc=16)
        P.op("dve", lambda e: e.memset(tmp[2][:], 1.0), writes=[c_tmp[2]])
        P.op("dve", lambda e: e.tensor_copy(r_(ones[:]), tmp[2][:, 0:128]), reads=[c_tmp[2]], writes=[c_ones])
        P.op("dve", lambda e: e.memset(tmp[3][:], 0.0), writes=[c_tmp[3]])
        P.op("dve", lambda e: e.memset(halo[:], 0.0), writes=[c for l in range(NL) for c in c_halo[l]])
        P.op("dve", lambda e: e.memset(schalo[:], 0.0), writes=[c for l in range(NL) for c in c_schalo[l]])
        P.op("dve", lambda e: e.memset(hst[:], 0.0), writes=[c for l in range(NL) for c in c_hst[l]])
        for l in range(NL):
            for h in range(4):
                for dc in range(2):
                    P.op("dve", (lambda l, h, dc: lambda e: e.tensor_copy(r_(rst[l][h][:, dc, :]), tmp[3][:]))(l, h, dc), reads=[c_tmp[3]], writes=[c_rst[l][h]])
        for l in range(NL):
            for j, nm in enumerate(["ffn1_post", "ffn2_post"]):
                o = l * NPC + PO[nm]
                dve_ts(ghalf[:, l, j, :], pp[:, o:o + KC], 0.5, None, ALU.mult, None, [c_pp], [c_der])
            o = l * NPC + PO["lam"]
            t0 = tmp[0][:, 0:KC]
            act(t0, pp[:, o:o + KC], AF.Exp, [c_pp], [c_tmp[0]], scale=-1.0)
            act(t0, t0, AF.Ln, [c_tmp[0]], [c_tmp[0]], bias=1.0)
            dve_ts(nsp[:, l, 0, :], t0, -8.0, None, ALU.mult, None, [c_tmp[0]], [c_der])
            dve_ts(nsp[:, l, 1, :], t0, -16.0, None, ALU.mult, None, [c_tmp[0]], [c_der])

        def norm_stats(src_fn, src_cells):
            b = nb()
            for kc in range(KC):
                t = ntr()
                act(r_(tmpR[t][:]), src_fn(kc), AF.Square, [src_cells[kc]], [c_tmpR[t]])
                P.op("pe", (lambda kc, t: lambda e: e.matmul(banks[b][:], r_(ones[:]), r_(tmpR[t][:]), start=(kc == 0), stop=(kc == KC - 1)))(kc, t),
                     reads=[c_ones, c_tmpR[t]], writes=[c_bank[b]])
            act(rstd[:], banks[b][:], AF.Ln, [c_bank[b]], [c_rstd], scale=1.0 / D, bias=EPS_AP[:])
            act(rstd[:], rstd[:], AF.Exp, [c_rstd], [c_rstd], scale=-0.5)

        def pre_norm(l, gname):
            norm_stats(lambda kc: xres[:, kc, :], c_x)
            for kc in range(KC):
                dve_stt(r_(ubuf[:, kc, :]), xres[:, kc, :], ppcol(l, gname, kc), rstd[:], ALU.mult, ALU.mult,
                        [c_x[kc], c_pp, c_rstd], [c_u[kc]])

        def pre_norm_deferred(l, gname):
            for kc in range(KC):
                act(r_(ubuf[:, kc, :]), xres[:, kc, :], AF.Copy, [c_x[kc], c_pp], [c_u[kc]], scale=ppcol(l, gname, kc))
            norm_stats(lambda kc: xres[:, kc, :], c_x)

        def post_norm_add(gcol_fn):
            norm_stats(lambda kc: ubuf[:, kc, :], c_u)
            for kc in range(KC):
                t = nt()
                dve_stt(tmp[t][:], ubuf[:, kc, :], gcol_fn(kc), rstd[:], ALU.mult, ALU.mult,
                        [c_u[kc], c_pp, c_der, c_rstd], [c_tmp[t]])
                dve_tt(xres[:, kc, :], xres[:, kc, :], tmp[t][:], ALU.add, [c_tmp[t], c_x[kc]], [c_x[kc]])

        def ffn(l, pre, win, wout, ghalf_j, mid_hook=None):
            state["pool"] = POOL_FFN
            pre_norm_deferred(l, pre)
            halves = [list(range(0, 12)), list(range(12, 22))]
            for hi, chunks in enumerate(halves):
                base = chunks[0]
                for j0 in range(chunks[0], chunks[-1] + 1, 2):
                    gv, gc = wload(win[l][:, j0 * 128:j0 * 128 + 256], KC, 256)
                    uv, uc = wload(win[l][:, DFF + j0 * 128:DFF + j0 * 128 + 256], KC, 256)
                    for jj in range(2):
                        bg, bu = nb(), nb()
                        gterms = [(r_(gv[:, kc, jj * 128:(jj + 1) * 128]), r_(ubuf[:, kc, :])) for kc in range(KC)]
                        if hi == 0 and j0 == 0 and jj == 0:
                            mmstream(bg, gterms, gc)
                        else:
                            mmgroup(bg, 0, TT, gterms, gc + c_u)
                        mmgroup(bu, 0, TT, [(r_(uv[:, kc, jj * 128:(jj + 1) * 128]), r_(ubuf[:, kc, :])) for kc in range(KC)], uc + c_u)
                        t, t2 = nt(), nt()
                        dve_tt(tmp[t][:], banks[bg][:], rstd[:], ALU.mult, [c_bank[bg], c_rstd], [c_tmp[t]])
                        act(tmp[t][:], tmp[t][:], AF.Silu, [c_tmp[t]], [c_tmp[t]])
                        dve_tt(tmp[t2][:], banks[bu][:], rstd[:], ALU.mult, [c_bank[bu], c_rstd], [c_tmp[t2]])
                        j = j0 + jj - base
                        dve_tt(r_(big[:, j, :]), tmp[t][:], tmp[t2][:], ALU.mult, [c_tmp[t], c_tmp[t2]], [c_big[j]])
                if hi == 0 and mid_hook is not None:
                    mid_hook()
                if hi == 1:
                    act(ssq[:, 7:8], ONE_AP[:], AF.Ln, [c_k], [c_ssq[7]])
                nk = len(chunks)
                ksl = [(0, 8), (8, nk)]
                for npair in range(4):
                    b2 = [nb(), nb()]
                    for si, (k0, k1) in enumerate(ksl):
                        wv, wc = wload(wout[l][(base + k0) * 128:(base + k1) * 128, npair * 256:(npair + 1) * 256], k1 - k0, 256)
                        for nn in range(2):
                            def fn(e, wv=wv, nn=nn, k0=k0, k1=k1, b=b2[nn], si=si):
                                ins = None
                                for k in range(k0, k1):
                                    ins = e.matmul(banks[b][:], r_(wv[:, k - k0, nn * 128:(nn + 1) * 128]), r_(big[:, k, :]),
                                                   start=(si == 0 and k == k0), stop=(si == 1 and k == k1 - 1))
                                return ins
                            P.op("pe", fn, reads=wc + c_big[k0:k1], writes=[c_bank[b2[nn]]])
                    for nn in range(2):
                        n = npair * 2 + nn
                        if hi == 0:
                            act(r_(Sb[:, n, :]), banks[b2[nn]][:], AF.Copy, [c_bank[b2[nn]]], [c_S[n]])
                        else:
                            dve_tt(r_(ubuf[:, n, :]), banks[b2[nn]][:], Sb[:, n, :], ALU.add, [c_bank[b2[nn]], c_S[n]], [c_u[n]])
            post_norm_add(lambda kc: ghalf[:, l, ghalf_j, kc:kc + 1])

        def rope_tables(s):
            P.op("sp", lambda e: e.dma_start(out=posi[:], in_=pos[:, s * TT:(s + 1) * TT].partition_broadcast(128)),
                 writes=[c_posi], key="dpos", inc=16)
            a, b, c = nt(), nt(), nt()
            ang, t1, t2 = tmp[a], tmp[b], tmp[c]
            ca, c1, c2 = c_tmp[a], c_tmp[b], c_tmp[c]
            C1 = 6.28125; C2 = 2 * math.pi - 6.28125
            dve_copy(t1[:], posi[:], [c_posi], [c1])
            dve_ts(ang[:], t1[:], cst[:, C_INV:C_INV + 1], None, ALU.mult, None, [c1, c_cst], [ca])
            dve_ts(t1[:], ang[:], 1.0 / (2 * math.pi), None, ALU.mult, None, [ca], [c1])
            dve_copy(ki[:], t1[:], [c1], [c_ki])
            dve_copy(t1[:], ki[:], [c_ki], [c1])
            dve_stt(t2[:], t1[:], -C1, ang[:], ALU.mult, ALU.add, [c1, ca], [c2])
            dve_stt(ang[:], t1[:], -C2, t2[:], ALU.mult, ALU.add, [c1, c2], [ca])

            def wrap(buf, cb, scratch, cs_):
                dve_ts(scratch[:], buf[:], math.pi, -2 * math.pi, ALU.is_gt, ALU.mult, [cb], [cs_])
                dve_tt(buf[:], buf[:], scratch[:], ALU.add, [cs_, cb], [cb])
                dve_ts(scratch[:], buf[:], -math.pi, 2 * math.pi, ALU.is_lt, ALU.mult, [cb], [cs_])
                dve_tt(buf[:], buf[:], scratch[:], ALU.add, [cs_, cb], [cb])
            wrap(ang, ca, t1, c1)
            act(sinT[:], ang[:], AF.Sin, [ca], [c_cs])
            dve_ts(t2[:], ang[:], math.pi / 2, None, ALU.add, None, [ca], [c2])
            wrap(t2, c2, t1, c1)
            act(cosT[:], t2[:], AF.Sin, [c2], [c_cs])

        def gated_acc(l, gate_off, wname, nkc, first):
            win = W["w_mix_in"]
            for npair in range(4):
                b2 = [nb(), nb()]
                nsl = nkc // 8
                for si in range(nsl):
                    wv, wc = wload(W[wname][l][si * 1024:(si + 1) * 1024, npair * 256:(npair + 1) * 256], 8, 256)
                    for nn in range(2):
                        def fn(e, wv=wv, nn=nn, si=si, b=b2[nn]):
                            ins = None
                            for k in range(8):
                                ins = e.matmul(banks[b][:], r_(wv[:, k, nn * 128:(nn + 1) * 128]), r_(big[:, si * 8 + k, :]),
                                               start=(si == 0 and k == 0), stop=(si == nsl - 1 and k == 7))
                            return ins
                        P.op("pe", fn, reads=wc + c_big[si * 8:(si + 1) * 8], writes=[c_bank[b2[nn]]])
                gv, gc = wload(win[l][:, gate_off + npair * 256:gate_off + (npair + 1) * 256], KC, 256)
                for nn in range(2):
                    n = npair * 2 + nn
                    bg = nb()
                    mmgroup(bg, 0, TT, [(r_(gv[:, kc, nn * 128:(nn + 1) * 128]), r_(ubuf[:, kc, :])) for kc in range(KC)], gc + c_u)
                    t = nt()
                    act(tmp[t][:], banks[bg][:], AF.Sigmoid, [c_bank[bg]], [c_tmp[t]])
                    if first:
                        dve_tt(r_(Sb[:, n, :]), tmp[t][:], banks[b2[nn]][:], ALU.mult, [c_tmp[t], c_bank[b2[nn]]], [c_S[n]])
                    else:
                        dve_tt(tmp[t][:], tmp[t][:], banks[b2[nn]][:], ALU.mult, [c_bank[b2[nn]]], [c_tmp[t]])
                        dve_tt(r_(Sb[:, n, :]), Sb[:, n, :], tmp[t][:], ALU.add, [c_tmp[t]], [c_S[n]])

        def branch_A(l):
            win = W["w_mix_in"]
            xr = [xraw[:, :], rtok[:, 2:4, :].rearrange("p a b -> p (a b)")[:, 0:TT + 4]]
            cxr = [[c_xraw], [c_rtok[2], c_rtok[3]]]
            TB = [[(tmp[i][:], c_tmp[i]) for i in range(4)],
                  [(tmp[4][:], c_tmp[4]), (tmp[5][:], c_tmp[5]), (rtok[:, 0, :], c_rtok[0]), (rtok[:, 1, :], c_rtok[1])]]
            TA = [(rstd[:], c_rstd), (posi[:].bitcast(F32), c_posi)]
            ST = {}

            def S1(p):
                c0 = 2 * p
                xv, xc = wload(win[l][:, c0 * 128:c0 * 128 + 256], KC, 256)
                yv, yc = wload(win[l][:, D + c0 * 128:D + c0 * 128 + 256], KC, 256)
                P.op("pool", (lambda c0=c0: lambda e: e.dma_start(out=r_(wgb[:]), in_=wbd_d[l].rearrange("p (g k n) -> p g k n", g=2, k=KC)[:, :, c0:c0 + 2, :]))(),
                     writes=[c_wgb], key="wg", inc=16)
                bxs, bys, brs, bis = [], [], [], []
                for cc in range(2):
                    bx = nb(); bxs.append(bx)
                    xterms = [(r_(xv[:, kc, cc * 128:(cc + 1) * 128]), r_(ubuf[:, kc, :])) for kc in range(KC)]
                    if p == 0 and cc == 0:
                        mmstream(bx, xterms, xc)
                    else:
                        mmgroup(bx, 0, TT, xterms, xc + c_u)
                for cc in range(2):
                    by = nb(); bys.append(by)
                    mmgroup(by, 0, TT, [(r_(yv[:, kc, cc * 128:(cc + 1) * 128]), r_(ubuf[:, kc, :])) for kc in range(KC)], yc + c_u)
                for cc in range(2):
                    c = c0 + cc
                    ta, cta = TA[cc]
                    X = xr[cc]; cX = cxr[cc]
                    dve_copy(X[:, 0:3], halo[:, l, c, 0:3], [c_halo[l][c]], cX)
                    dve_copy(X[:, 3:3 + TT], banks[bxs[cc]][:], [c_bank[bxs[cc]]], cX)
                    dve_copy(halo[:, l, c, 0:3], X[:, TT:TT + 3], cX, [c_halo[l][c]])
                    cw = lambda k: ppcol(l, "conv_w", c, k)
                    dve_ts(ta, X[:, 3:3 + TT], cw(3), ppcol(l, "conv_b", c), ALU.mult, ALU.add, cX + [c_pp], [cta])
                    dve_stt(ta, X[:, 2:2 + TT], cw(2), ta, ALU.mult, ALU.add, cX + [c_pp], [cta])
                    dve_stt(ta, X[:, 1:1 + TT], cw(1), ta, ALU.mult, ALU.add, cX + [c_pp], [cta])
                    dve_stt(r_(tmpR[cc][:]), X[:, 0:TT], cw(0), ta, ALU.mult, ALU.add, cX + [c_pp, cta], [c_tmpR[cc]])
                    br, bi = nb(), nb(); brs.append(br); bis.append(bi)
                    mmgroup(br, 0, TT, [(r_(wgb[:, 0, cc, :]), r_(tmpR[cc][:]))], [c_wgb, c_tmpR[cc]])
                    mmgroup(bi, 0, TT, [(r_(wgb[:, 1, cc, :]), r_(tmpR[cc][:]))], [c_wgb, c_tmpR[cc]])
                ST[p] = (bys, brs, bis)

            def S2a(p):
                c0 = 2 * p
                bys, brs, bis = ST[p]
                for cc in range(2):
                    c = c0 + cc
                    act(r_(big[:, c, :]), banks[bys[cc]][:], AF.Gelu_apprx_tanh, [c_bank[bys[cc]]], [c_big[c]])
                for cc in range(2):
                    c = c0 + cc
                    (tr, ctr), (ti, cti), (tA, ctA), (tm, ctm) = TB[cc]
                    act(tr, banks[brs[cc]][:], AF.Sigmoid, [c_bank[brs[cc]], c_pp], [ctr], bias=ppcol(l, "b_a", c))
                    act(ti, banks[bis[cc]][:], AF.Sigmoid, [c_bank[bis[cc]], c_pp], [cti], bias=ppcol(l, "b_x", c))
                for cc in range(2):
                    c = c0 + cc
                    (tr, ctr), (ti, cti), (tA, ctA), (tm, ctm) = TB[cc]
                    act(tA, tr, AF.Exp, [ctr, c_der], [ctA], scale=nsp[:, l, 0, c:c + 1])
                for cc in range(2):
                    (tr, ctr), (ti, cti), (tA, ctA), (tm, ctm) = TB[cc]
                    dve_tt(ti, ti, tmpR[cc][:], ALU.mult, [c_tmpR[cc]], [cti])
                    dve_tt(tm, tA, tA, ALU.mult, [ctA], [ctm])

            def S2b(p):
                c0 = 2 * p
                for cc in range(2):
                    (tr, ctr), (ti, cti), (tA, ctA), (tm, ctm) = TB[cc]
                    act(tm, tm, AF.Sqrt, [ctm], [ctm], scale=-1.0, bias=ONE_AP[:])
                for cc in range(2):
                    c = c0 + cc
                    (tr, ctr), (ti, cti), (tA, ctA), (tm, ctm) = TB[cc]
                    dve_tt(ti, ti, tm, ALU.mult, [ctm], [cti])
                    P.op("dve", (lambda tr=tr, tA=tA, ti=ti, c=c: lambda e: e.tensor_tensor_scan(tr, tA, ti, hst[:, l, c:c + 1], ALU.mult, ALU.add))(),
                         reads=[ctA, cti, c_hst[l][c]], writes=[ctr])
                    dve_copy(hst[:, l, c:c + 1], tr[:, TT - 1:TT], [ctr], [c_hst[l][c]])
                    dve_tt(r_(big[:, c, :]), tr, big[:, c, :], ALU.mult, [ctr], [c_big[c]])

            S1(0)
            for p in range(4):
                S2a(p)
                if p < 3:
                    S1(p + 1)
                S2b(p)

        def branch_C(l):
            win = W["w_mix_in"]
            for c0 in range(0, KC, 2):
                bv, bc_ = wload(win[l][:, 8192 + c0 * 128:8192 + c0 * 128 + 256], KC, 256)
                cv, cc_ = wload(win[l][:, 9216 + c0 * 128:9216 + c0 * 128 + 256], KC, 256)
                hv, hc_ = wload(win[l][:, 10240 + c0 * 128:10240 + c0 * 128 + 256], KC, 256)
                for cc in range(2):
                    c = c0 + cc
                    b_b, b_c, b_h = nb(), nb(), nb()
                    for (b, v, vc) in [(b_c, cv, cc_), (b_h, hv, hc_), (b_b, bv, bc_)]:
                        mmgroup(b, 0, TT, [(r_(v[:, kc, cc * 128:(cc + 1) * 128]), r_(ubuf[:, kc, :])) for kc in range(KC)], vc + c_u)
                    t1, t2, t3 = nt(), nt(), nt()
                    act(tmp[t1][:], banks[b_c][:], AF.Copy, [c_bank[b_c]], [c_tmp[t1]])
                    act(tmp[t3][:], banks[b_b][:], AF.Copy, [c_bank[b_b]], [c_tmp[t3]])
                    dve_copy(xraw[:, 0:2], schalo[:, l, c, 0:2], [c_schalo[l][c]], [c_xraw])
                    dve_tt(xraw[:, 2:2 + TT], tmp[t1][:], banks[b_h][:], ALU.mult, [c_tmp[t1], c_bank[b_h]], [c_xraw])
                    dve_copy(schalo[:, l, c, 0:2], xraw[:, TT:TT + 2], [c_xraw], [c_schalo[l][c]])
                    sw = lambda k: ppcol(l, "sc_w", c, k)
                    dve_ts(tmp[t2][:], xraw[:, 2:2 + TT], sw(2), None, ALU.mult, None, [c_xraw, c_pp], [c_tmp[t2]])
                    dve_stt(tmp[t2][:], xraw[:, 1:1 + TT], sw(1), tmp[t2][:], ALU.mult, ALU.add, [c_xraw, c_pp], [c_tmp[t2]])
                    dve_stt(tmp[t2][:], xraw[:, 0:TT], sw(0), tmp[t2][:], ALU.mult, ALU.add, [c_xraw, c_pp], [c_tmp[t2]])
                    dve_tt(r_(big[:, c, :]), tmp[t2][:], tmp[t3][:], ALU.mult, [c_tmp[t2], c_tmp[t3]], [c_big[c]])

        def branch_B(l):
            win = W["w_mix_in"]
            ident = cst[:, C_ID:C_ID + 128]
            OFF = [0, 512, 896, 1152]
            ktv = kT[:].rearrange("p a b -> p (a b)").rearrange("p (b d) -> p b d", b=4)
            HS = {}

            def P1(h):
                GP = G64[h]
                for (dst, cdst, off) in [(qT, c_qT, 2048 + h * 256), (kT, c_kT, 3072 + h * 256)]:
                    wv, wc = wload(win[l][:, off:off + 256], KC, 256)
                    b0, b1 = nb(), nb()
                    for dc, b in enumerate([b0, b1]):
                        mmgroup(b, 0, TT, [(r_(wv[:, kc, dc * 128:(dc + 1) * 128]), r_(ubuf[:, kc, :])) for kc in range(KC)], wc + c_u)
                    t1, t2 = nt(), nt()
                    dve_tt(tmp[t1][:], banks[b0][:], cosT[:], ALU.mult, [c_bank[b0], c_cs], [c_tmp[t1]])
                    dve_tt(tmp[t2][:], banks[b1][:], sinT[:], ALU.mult, [c_bank[b1], c_cs], [c_tmp[t2]])
                    dve_tt(r_(dst[:, 0, :]), tmp[t1][:], tmp[t2][:], ALU.subtract, [c_tmp[t1], c_tmp[t2]], [cdst])
                    dve_tt(tmp[t1][:], banks[b0][:], sinT[:], ALU.mult, [c_bank[b0], c_cs], [c_tmp[t1]])
                    dve_tt(tmp[t2][:], banks[b1][:], cosT[:], ALU.mult, [c_bank[b1], c_cs], [c_tmp[t2]])
                    dve_tt(r_(dst[:, 1, :]), tmp[t1][:], tmp[t2][:], ALU.add, [c_tmp[t1], c_tmp[t2]], [cdst])

            def P2(h):
                GP = G64[h]
                bsc = []
                for jb in range(4):
                    bs = nb(); bsc.append(bs)

                    def fn_sc(e, bs=bs, jb=jb):
                        ins = None
                        for dc in range(2):
                            ins = e.matmul(banks[bs][:, jb * 128:TT], r_(kT[:, dc, jb * 128:(jb + 1) * 128]),
                                           r_(qT[:, dc, jb * 128:TT]), start=(dc == 0), stop=(dc == 1))
                        return ins
                    P.op("pe", fn_sc, reads=[c_qT, c_kT], writes=[c_bank[bs]])
                bts = []
                for half in range(2):
                    bt = nb(); bts.append(bt)

                    def fn_t(e, bt=bt, half=half):
                        ins = None
                        for bb in range(2):
                            blk = half * 2 + bb
                            for dc in range(2):
                                ins = e.transpose(banks[bt][:, bb * 256 + dc * 128:bb * 256 + (dc + 1) * 128],
                                                  kT[:, dc, blk * 128:(blk + 1) * 128], ident)
                        return ins
                    P.op("pe", fn_t, reads=[c_kT, c_cst], writes=[c_bank[bt]])
                mkD = cst[:, C_MASK + h * 128:C_MASK + (h + 1) * 128]
                mkB = cst[:, C_BASE + h * 128:C_BASE + (h + 1) * 128]
                for jb in range(4):
                    bs = bsc[jb]
                    dve_tt(r_(S2[:, OFF[jb]:OFF[jb] + 128]), banks[bs][:, jb * 128:(jb + 1) * 128], mkD, ALU.mult, [c_bank[bs], c_cst], [c_S2])
                    for ib in range(jb + 1, 4):
                        d = ib - jb
                        dve_stt(r_(S2[:, OFF[jb] + d * 128:OFF[jb] + (d + 1) * 128]), banks[bs][:, ib * 128:(ib + 1) * 128], GP[d], mkB,
                                ALU.mult, ALU.mult, [c_bank[bs], c_cst], [c_S2])
                for jb in range(4):
                    half, bb = jb // 2, jb % 2
                    act(r_(ktv[:, jb, :]), banks[bts[half]][:, bb * 256:(bb + 1) * 256], AF.Copy, [c_bank[bts[half]], c_cst], [c_kT],
                        scale=cst[:, C_KD + h * 4 + jb:C_KD + h * 4 + jb + 1])
                bvs = [nb() for _ in range(4)]
                for hf in range(2):
                    wv, wc = wload(win[l][:, 4096 + h * 512 + hf * 256:4096 + h * 512 + (hf + 1) * 256], KC, 256)
                    for blk in range(4):
                        mmgroup(bvs[blk], hf * 256, 256, [(r_(ubuf[:, kc, blk * 128:(blk + 1) * 128]), r_(wv[:, kc, :])) for kc in range(KC)], wc + c_u)
                for blk in range(4):
                    act(r_(vtok[:, blk, :]), banks[bvs[blk]][:], AF.Copy, [c_bank[bvs[blk]]], [c_vtok[blk]])
                qtab = cst[:, C_QD + h * 128:C_QD + (h + 1) * 128]
                for ib in range(4):
                    P.op("dve", (lambda ib=ib, qtab=qtab, GP=GP: lambda e: e.scalar_tensor_tensor(
                        r_(qd[:, :, ib * 128:(ib + 1) * 128]), qT[:, :, ib * 128:(ib + 1) * 128], GP[ib],
                        qtab.unsqueeze(1).to_broadcast([128, 2, 128]), ALU.mult, ALU.mult))(), reads=[c_qT, c_cst], writes=[c_qd])
                S0 = rst[l][h]; cS0 = c_rst[l][h]
                bq = [nb(), nb()]
                for dc in range(2):
                    mmgroup(bq[dc], 0, TT, [(r_(ktv[:, jb, dc * 128:(dc + 1) * 128]), r_(vtok[:, jb, :])) for jb in range(4)], [c_kT] + c_vtok)
                HS[h] = (bq,)
                HELD.update(bq)

            def P3(h):
                GP = G64[h]
                S0 = rst[l][h]; cS0 = c_rst[l][h]
                (bq,) = HS[h]
                sb0 = (h % 2) * 4
                bos = []
                for ib in range(4):
                    bo = nb(); bos.append(bo)
                    terms = [(r_(S2[:, OFF[jb] + (ib - jb) * 128:OFF[jb] + (ib - jb + 1) * 128]), r_(vtok[:, jb, :])) for jb in range(ib + 1)]
                    terms += [(r_(qd[:, dc, ib * 128:(ib + 1) * 128]), r_(S0[:, dc, :])) for dc in range(2)]
                    mmgroup(bo, 0, TT, terms, [c_S2, c_qd, cS0] + c_vtok[0:ib + 1])
                    t = nt()
                    act(tmp[t][:], banks[bo][:], AF.Square, [c_bank[bo]], [c_tmp[t], c_ssq[sb0 + ib]], accum_out=ssq[:, sb0 + ib:sb0 + ib + 1])
                act(ssq[:, sb0:sb0 + 4], ssq[:, sb0:sb0 + 4], AF.Ln, c_ssq[sb0:sb0 + 4], c_ssq[sb0:sb0 + 4], scale=1.0 / 512, bias=EPS_AP[:])
                act(ssq[:, sb0:sb0 + 4], ssq[:, sb0:sb0 + 4], AF.Exp, c_ssq[sb0:sb0 + 4], c_ssq[sb0:sb0 + 4], scale=-0.5)
                for ib in range(4):
                    act(rtok[:, ib, :], banks[bos[ib]][:], AF.Copy, [c_bank[bos[ib]], c_ssq[sb0 + ib]], [c_rtok[ib]], scale=ssq[:, sb0 + ib:sb0 + ib + 1])
                for dc in range(2):
                    dve_stt(r_(S0[:, dc, :]), S0[:, dc, :], GP[4], banks[bq[dc]][:], ALU.mult, ALU.add, [cS0, c_bank[bq[dc]]], [cS0])
                HELD.difference_update(bq)
                GS = {}

                def Gp(ec):
                    gh, e2 = ec // 2, ec % 2
                    if e2 == 0:
                        GS["w"] = wload(win[l][:, 6144 + h * 512 + gh * 256:6144 + h * 512 + (gh + 1) * 256], KC, 256)
                    wv, wc = GS["w"]
                    bg = nb()
                    mmgroup(bg, 0, TT, [(r_(wv[:, kc, e2 * 128:(e2 + 1) * 128]), r_(ubuf[:, kc, :])) for kc in range(KC)], wc + c_u)
                    t = nt()
                    act(tmp[t][:], banks[bg][:], AF.Silu, [c_bank[bg]], [c_tmp[t]])
                    GS[ec] = t

                def Tp(ec):
                    t = GS[ec]
                    bt = nb()

                    def fn_t2(e, bt=bt, ec=ec):
                        ins = None
                        for blk in range(4):
                            ins = e.transpose(banks[bt][:, blk * 128:(blk + 1) * 128], rtok[:, blk, ec * 128:(ec + 1) * 128], ident)
                        return ins
                    P.op("pe", fn_t2, reads=c_rtok + [c_cst], writes=[c_bank[bt]])
                    dve_tt(r_(big[:, h * 4 + ec, :]), tmp[t][:], banks[bt][:], ALU.mult, [c_tmp[t], c_bank[bt]], [c_big[h * 4 + ec]])
                Gp(0); Gp(1); Tp(0); Gp(2); Tp(1); Gp(3); Tp(2); Tp(3)

            P1(0)
            for h in range(4):
                P2(h)
                if h < 3:
                    P1(h + 1)
                P3(h)

        def dump_big(s, n):
            P.op("sp", lambda e: e.dma_start(out=dbg[s][:, 0:n, :], in_=big[:, 0:n, :]), reads=c_big[0:n], key="ddbg", inc=16)

        def dump_S(s):
            P.op("sp", lambda e: e.dma_start(out=dbg[s][:, 0:KC, :], in_=Sb[:]), reads=c_S, key="ddbg", inc=16)

        def mixer(l, s):
            state["pool"] = POOL_MIX
            pre_norm(l, "mix_pre")
            if dump == "U":
                P.op("sp", lambda e: e.dma_start(out=dbg[s][:, 0:KC, :], in_=ubuf[:]), reads=c_u, key="ddbg", inc=16)
                return False
            if dump in (None, "A", "S"):
                branch_A(l)
                if dump == "A":
                    dump_big(s, 8); return False
                gated_acc(l, 11264, "w_lru_out", 8, True)
            if dump in (None, "B", "S"):
                state["pool"] = POOL_B
                branch_B(l)
                state["pool"] = POOL_MIX
                if dump == "B":
                    dump_big(s, 16); return False
                gated_acc(l, 12288, "w_ret_out", 16, False)
            if dump in (None, "C", "S"):
                state["pool"] = POOL_C
                branch_C(l)
                if dump == "C":
                    dump_big(s, 8); return False
                gated_acc(l, 13312, "w_sc_out", 8, False)
            if dump == "S":
                dump_S(s); return False
            act(ssq[:, 7:8], ONE_AP[:], AF.Ln, [c_k], [c_ssq[7]])
            for npair in range(4):
                wv, wc = wload(W["w_mix_out"][l][:, npair * 256:(npair + 1) * 256], KC, 256)
                for nn in range(2):
                    n = npair * 2 + nn
                    b = nb()
                    mmgroup(b, 0, TT, [(r_(wv[:, kc, nn * 128:(nn + 1) * 128]), r_(Sb[:, kc, :])) for kc in range(KC)], wc + c_S)
                    act(r_(ubuf[:, n, :]), banks[b][:], AF.Copy, [c_bank[b]], [c_u[n]])
            post_norm_add(lambda kc: ppcol(l, "mix_post", kc))
            return True

        EPS_AP = sb("eps_ap", [128, 1]); ONE_AP = sb("one_ap", [128, 1]); c_k = Cell("k")
        P.op("dve", lambda e: e.memset(EPS_AP[:], EPS), writes=[c_k])
        P.op("dve", lambda e: e.memset(ONE_AP[:], 1.0), writes=[c_k])
        P.op("act", lambda e: e.activation(tmp[1][:, 0:1], EPS_AP[:], AF.Copy), reads=[c_k], writes=[c_tmp[1]])

        out_toks = []
        for s in range(NT):
            P.op("sp", (lambda s: lambda e: e.dma_start(out=xres[:], in_=xTv[:, :, s * TT:(s + 1) * TT]))(s), writes=c_x, key="dxin", inc=16)
            go = True
            for l in range(n_layers):
                if not go:
                    break
                ffn(l, "ffn1_pre", W["ffn1_w_in"], W["ffn1_w_out"], 0, mid_hook=((lambda s=s: rope_tables(s)) if l == 0 else None))
                if stop == (l, "ffn1"):
                    break
                go = mixer(l, s)
                if not go or stop == (l, "mix"):
                    break
                ffn(l, "ffn2_pre", W["ffn2_w_in"], W["ffn2_w_out"], 1)
                if stop == (l, "ffn2"):
                    break
            out_toks.append(P.op("sp", (lambda s: lambda e: e.dma_start(out=oTv[:, :, s * TT:(s + 1) * TT], in_=xres[:]))(s), reads=c_x, key="dxout", inc=16))
        fin = [out_toks[-1]]
        if dump:
            fin.append(("ddbg", P.cnt["ddbg"]))
        P.wait_all("sp", fin)

        sems = {}
        for k in sorted(P.cnt.keys()):
            sems[k] = st.enter_context(nc.semaphore("s_" + k))
        block = st.enter_context(nc.Block())

        @block.tensor
        def _(e):
            P.replay("pe", e, sems)

        @block.scalar
        def _(e):
            P.replay("act", e, sems)

        @block.vector
        def _(e):
            P.replay("dve", e, sems)

        @block.gpsimd
        def _(e):
            P.replay("pool", e, sems)

        @block.sync
        def _(e):
            P.replay("sp", e, sems)
    return nc


def pack_inputs(inputs):
    L = NL

    def pk(v):
        return np.ascontiguousarray(np.asarray(v, np.float32).reshape(KC, 128).T)
    pp = np.zeros((128, L * NPC), np.float32)
    for l in range(L):
        o = l * NPC
        for nm, key in [("ffn1_pre", "ffn1_pre_g"), ("ffn1_post", "ffn1_post_g"), ("mix_pre", "mix_pre_g"), ("mix_post", "mix_post_g"),
                        ("ffn2_pre", "ffn2_pre_g"), ("ffn2_post", "ffn2_post_g"), ("conv_b", "lru_conv_b"), ("b_a", "lru_b_a"),
                        ("b_x", "lru_b_x"), ("lam", "lru_lambda")]:
            pp[:, o + PO[nm]:o + PO[nm] + KC] = pk(inputs[key][l])
        for k in range(4):
            pp[:, o + PO["conv_w"] + k * KC:o + PO["conv_w"] + (k + 1) * KC] = pk(inputs["lru_conv_w"][l][k])
        for k in range(3):
            pp[:, o + PO["sc_w"] + k * KC:o + PO["sc_w"] + (k + 1) * KC] = pk(inputs["sc_conv_w"][l][k])
    wbd = np.zeros((L, 128, 2, KC, 128), np.float32)
    for l in range(L):
        for g, key in enumerate(["lru_w_a", "lru_w_x"]):
            w = np.asarray(inputs[key][l], np.float32)
            for c in range(KC):
                wbd[l, 0:64, g, c, 0:64] = w[2 * c]
                wbd[l, 64:128, g, c, 64:128] = w[2 * c + 1]
    wbd = wbd.reshape(L, 128, 2 * KC * 128)
    cst, _ = head_consts()
    common = {"pos": np.asarray(inputs["positions"], np.int32).reshape(1, SEQ), "pp": pp, "cst": cst, "wbd": wbd}
    for nm in ["ffn1_w_in", "ffn1_w_out", "w_mix_in", "w_lru_out", "w_ret_out", "w_sc_out", "w_mix_out", "ffn2_w_in", "ffn2_w_out"]:
        common[nm] = np.ascontiguousarray(np.asarray(inputs[nm], np.float32))
    return common


_NC_CACHE = {}


def kernel(**inputs):
    x = np.asarray(inputs["x"], np.float32)
    B = x.shape[0]
    common = pack_inputs(inputs)
    in_maps = []
    for b in range(B):
        m = dict(common)
        m["xT"] = np.ascontiguousarray(x[b].T)
        in_maps.append(m)
    if "nc" not in _NC_CACHE:
        _NC_CACHE["nc"] = build()
    res = run_bass_kernel_spmd(_NC_CACHE["nc"], in_maps, core_ids=list(range(B)))
    out = np.stack([np.ascontiguousarray(r["outT"].T) for r in res.results], 0)
    return out.astype(np.float32)
```

```python
import contextlib
import math
import numpy as np
import concourse.bass as bass
import concourse.mybir as mybir
from concourse.bass_utils import run_bass_kernel_spmd

F32 = mybir.dt.float32
F32R = mybir.dt.float32r
I32 = mybir.dt.int32
AF = mybir.ActivationFunctionType
ALU = mybir.AluOpType

D = 1024
KC = 8
SEQ = 2048
TT = 512
NT = SEQ // TT
DFF = 2816
NFF = 22
MIXIN = 14336
NL = 2
EPS = 1e-6
NSLOT = 3
NPC = 136

ENGS = ["pe", "act", "dve", "pool", "sp"]
SELF_SYNC = {"pe": False, "act": True, "dve": True, "pool": True, "sp": False}


class Cell:
    __slots__ = ("name", "w", "r")

    def __init__(self, name):
        self.name = name
        self.w = None
        self.r = {}


def cells(name, n):
    return [Cell(f"{name}{i}") for i in range(n)]


class Prog:
    def __init__(self):
        self.ops = {e: [] for e in ENGS}
        self.cnt = {}
        self.seen = {e: {} for e in ENGS}

    def op(self, eng, fn, reads=(), writes=(), key=None, inc=1):
        need = {}

        def add(tok):
            if tok is None:
                return
            k, v = tok
            if need.get(k, 0) < v:
                need[k] = v

        for c in reads:
            add(c.w)
        for c in writes:
            add(c.w)
            for k, v in c.r.items():
                add((k, v))
        waits = []
        seen = self.seen[eng]
        for k, v in need.items():
            if k == eng and not SELF_SYNC[eng]:
                continue
            if seen.get(k, 0) < v:
                seen[k] = v
                waits.append((k, v))
        k = key or eng
        self.cnt[k] = self.cnt.get(k, 0) + inc
        tok = (k, self.cnt[k])
        for c in reads:
            if c.r.get(k, 0) < tok[1]:
                c.r[k] = tok[1]
        for c in writes:
            c.w = tok
            c.r = {}
        self.ops[eng].append((waits, fn, k, inc))
        return tok

    def wait_all(self, eng, toks):
        self.ops[eng].append((list(toks), None, None, 0))

    def replay(self, eng, e, sems):
        for waits, fn, k, inc in self.ops[eng]:
            for (wk, wv) in waits:
                e.wait_ge(sems[wk], wv)
            if fn is None:
                continue
            ins = fn(e)
            ins.then_inc(sems[k], inc)


def r_(ap):
    return ap.bitcast(F32R)


def head_consts():
    H = 4
    idx = np.arange(128)
    cols = []
    inv_freq = np.power(np.float32(10000.0), -np.arange(128, dtype=np.float32) / np.float32(128)).astype(np.float32)
    cols.append(inv_freq[:, None])
    gam = [1.0 - 2.0 ** (-5.0 - h) for h in range(H)]
    jj, ii = idx[:, None].astype(np.float64), idx[None, :].astype(np.float64)
    cj, ci = idx[:, None] // 64, idx[None, :] // 64
    for h in range(H):
        g = gam[h]
        m = np.where(cj == ci, g ** np.abs(ii - jj), np.where(cj < ci, g ** (ii - jj), 0.0)) / 16.0
        cols.append(m.astype(np.float32))
    for h in range(H):
        cols.append((gam[h] ** (ii - jj) / 16.0).astype(np.float32))
    for h in range(H):
        v = (gam[h] ** (idx + 1.0)).astype(np.float32)
        cols.append(np.tile(v[None, :], (128, 1)))
    kd = np.stack([gam[h] ** (511.0 - jb * 128.0 - idx) / 16.0 for h in range(H) for jb in range(4)], 1).astype(np.float32)
    cols.append(kd)
    cols.append(np.eye(128, dtype=np.float32))
    cst = np.concatenate(cols, 1).astype(np.float32)
    gp = [{d: float(gam[h] ** (128.0 * d)) for d in range(5)} for h in range(H)]
    return cst, gp


C_INV = 0
C_MASK = 1
C_BASE = 513
C_QD = 1025
C_KD = 1537
C_ID = 1553
NCST = 1681

PO = dict(ffn1_pre=0, ffn1_post=8, mix_pre=16, mix_post=24, ffn2_pre=32, ffn2_post=40,
          conv_w=48, conv_b=80, b_a=88, b_x=96, lam=104, sc_w=112)


def build(n_layers=NL, stop=None, dump=None):
    cst_np, G64 = head_consts()
    nc = bass.Bass("TRN2", target_bir_lowering=False)
    xT = nc.dram_tensor("xT", [D, SEQ], F32, kind="ExternalInput").ap()
    pos = nc.dram_tensor("pos", [1, SEQ], I32, kind="ExternalInput").ap()
    pp_d = nc.dram_tensor("pp", [128, NL * NPC], F32, kind="ExternalInput").ap()
    cst_d = nc.dram_tensor("cst", [128, NCST], F32, kind="ExternalInput").ap()
    wbd_d = nc.dram_tensor("wbd", [NL, 128, 2 * KC * 128], F32, kind="ExternalInput").ap()
    W = {}
    for nm, shp in [("ffn1_w_in", [NL, D, 2 * DFF]), ("ffn1_w_out", [NL, DFF, D]), ("w_mix_in", [NL, D, MIXIN]),
                    ("w_lru_out", [NL, D, D]), ("w_ret_out", [NL, 2 * D, D]), ("w_sc_out", [NL, D, D]),
                    ("w_mix_out", [NL, D, D]), ("ffn2_w_in", [NL, D, 2 * DFF]), ("ffn2_w_out", [NL, DFF, D])]:
        W[nm] = nc.dram_tensor(nm, shp, F32, kind="ExternalInput").ap()
    outT = nc.dram_tensor("outT", [D, SEQ], F32, kind="ExternalOutput").ap()
    dbg = None
    if dump:
        dbg = nc.dram_tensor("dbg", [NT, 128, 16, TT], F32, kind="ExternalOutput").ap()
    xTv = xT.rearrange("(kc p) t -> p kc t", p=128)
    oTv = outT.rearrange("(kc p) t -> p kc t", p=128)

    P = Prog()
    with contextlib.ExitStack() as st:
        def sb(name, shape, dt=F32):
            return st.enter_context(nc.sbuf_tensor(name, shape, dt))

        xres = sb("xres", [128, KC, TT]); c_x = cells("x", KC)
        ubuf = sb("ubuf", [128, KC, TT]); c_u = cells("u", KC)
        big = sb("big", [128, 16, TT]); c_big = cells("big", 16)
        Sb = sb("Sb", [128, KC, TT]); c_S = cells("S", KC)
        slots = [sb(f"wslot{i}", [128, 2048]) for i in range(NSLOT)]; c_slot = cells("slot", NSLOT)
        NTMP = 6
        tmp = [sb(f"tmp{i}", [128, TT]) for i in range(NTMP)]; c_tmp = cells("tmp", NTMP)
        tmpR = [sb(f"tmpR{i}", [128, TT]) for i in range(2)]; c_tmpR = cells("tmpR", 2)
        xraw = sb("xraw", [128, TT + 4]); c_xraw = Cell("xraw")
        rstd = sb("rstd", [128, TT]); c_rstd = Cell("rstd")
        cosT = sb("cosT", [128, TT]); sinT = sb("sinT", [128, TT]); c_cs = Cell("cs")
        posi = sb("posi", [128, TT], I32); c_posi = Cell("posi")
        ki = posi; c_ki = c_posi
        pp = sb("pp_sb", [128, NL * NPC]); c_pp = Cell("pp")
        ghalf = sb("ghalf", [128, NL, 2, KC]); nsp = sb("nsp", [128, NL, 2, KC]); c_der = Cell("der")
        cst = sb("cst_sb", [128, NCST]); c_cst = Cell("cst")
        ones = sb("ones", [128, 128]); c_ones = Cell("ones")
        halo = sb("halo", [128, NL, KC, 4]); c_halo = [cells(f"halo{l}_", KC) for l in range(NL)]
        schalo = sb("schalo", [128, NL, KC, 2]); c_schalo = [cells(f"sch{l}_", KC) for l in range(NL)]
        hst = sb("hst", [128, NL, KC]); c_hst = [cells(f"hst{l}_", KC) for l in range(NL)]
        rst = [[sb(f"rst{l}_{h}", [128, 2, TT]) for h in range(4)] for l in range(NL)]
        c_rst = [[Cell(f"rst{l}_{h}") for h in range(4)] for l in range(NL)]
        qT = sb("qT", [128, 2, TT]); c_qT = Cell("qT")
        kT = sb("kT", [128, 2, TT]); c_kT = Cell("kT")
        vtok = sb("vtok", [128, 4, TT]); c_vtok = cells("vtok", 4)
        rtok = sb("rtok", [128, 4, TT]); c_rtok = cells("rtok", 4)
        S2 = sb("S2", [128, 1280]); c_S2 = Cell("S2")
        wgb = sb("wgb", [128, 2, 2, 128]); c_wgb = Cell("wgb")
        qd = sb("qd", [128, 2, TT]); c_qd = Cell("qd")
        ssq = sb("ssq", [128, 8]); c_ssq = cells("ssq", 8)
        banks = [st.enter_context(nc.psum_tensor(f"ps{i}", [128, TT], F32)) for i in range(8)]
        c_bank = cells("bank", 8)

        state = {"bank": 0, "slot": 0, "tmp": 0, "ssq": 0, "qeo": 0}
        SL = [(slots[i][:, :], [c_slot[i]], f"w{i}") for i in range(NSLOT)]
        SL_V = (vtok[:].rearrange("p a b -> p (a b)"), list(c_vtok), "wv")
        SL_B = (big[:, 12:16, :].rearrange("p a b -> p (a b)"), list(c_big[12:16]), "wb")
        SL_B2 = (big[:, 8:12, :].rearrange("p a b -> p (a b)"), list(c_big[8:12]), "wb2")
        POOL_FFN = SL + [SL_V, SL_B]
        POOL_C = SL + [SL_V, SL_B, SL_B2]
        POOL_MIX = SL + [SL_V]
        POOL_B = SL
        state["pool"] = POOL_FFN

        HELD = set()

        def nb():
            while True:
                b = state["bank"]; state["bank"] = (b + 1) % 8
                if b not in HELD:
                    return b

        def nt():
            t = state["tmp"]; state["tmp"] = (t + 1) % NTMP
            return t

        def ntr():
            t = state["ssq"] % 2; state["tmpr"] = state.get("tmpr", 0) + 1
            return state["tmpr"] % 2

        def wload(src, nk, ncols):
            pool = state["pool"]
            state["slot"] = (state["slot"] + 1) % len(pool)
            buf, cl, key = pool[state["slot"]]
            view = buf[:, 0:nk * ncols].rearrange("p (k n) -> p k n", k=nk)
            srcv = src.rearrange("(k p) n -> p k n", p=128)
            P.op("pool", lambda e: e.dma_start(out=r_(view), in_=srcv), writes=cl, key=key, inc=16)
            return view, cl

        def mmgroup(bank, col0, ncol, terms, reads):
            def fn(e):
                ins = None
                n = len(terms)
                for i, (l, r) in enumerate(terms):
                    ins = e.matmul(banks[bank][0:l.shape[-1], col0:col0 + ncol], l, r, start=(i == 0), stop=(i == n - 1))
                return ins
            P.op("pe", fn, reads=reads, writes=[c_bank[bank]])

        def mmstream(bank, terms, wcells):
            n = len(terms)
            for i, (l_, r) in enumerate(terms):
                P.op("pe", (lambda i=i, l_=l_, r=r: lambda e: e.matmul(banks[bank][:], l_, r, start=(i == 0), stop=(i == n - 1)))(),
                     reads=wcells + [c_u[i]], writes=[c_bank[bank]])

        def act(out, in_, func, reads, writes, **kw):
            P.op("act", lambda e: e.activation(out, in_, func, **kw), reads=reads, writes=writes)

        def dve_tt(out, a, b, op, reads, writes):
            P.op("dve", lambda e: e.tensor_tensor(out, a, b, op), reads=reads, writes=writes)

        def dve_ts(out, a, s1, s2, op0, op1, reads, writes):
            if s2 is None:
                P.op("dve", lambda e: e.tensor_scalar(out, a, s1, None, op0), reads=reads, writes=writes)
            else:
                P.op("dve", lambda e: e.tensor_scalar(out, a, s1, s2, op0, op1), reads=reads, writes=writes)

        def dve_stt(out, a, s, b, op0, op1, reads, writes):
            P.op("dve", lambda e: e.scalar_tensor_tensor(out, a, s, b, op0, op1), reads=reads, writes=writes)

        def dve_copy(out, a, reads, writes):
            P.op("dve", lambda e: e.tensor_copy(out, a), reads=reads, writes=writes)

        def ppcol(l, name, kc, j=0):
            o = l * NPC + PO[name] + j * KC + kc
            return pp[:, o:o + 1]

        P.op("sp", lambda e: e.dma_start(out=pp[:], in_=pp_d[:]), writes=[c_pp], key="dpp", inc=16)
        P.op("sp", lambda e: e.dma_start(out=cst[:], in_=cst_d[:]), writes=[c_cst], key="dcst", inc=16)
        P.op("dve", lambda e: e.memset(tmp[2][:], 1.0), writes=[c_tmp[2]])
        P.op("dve", lambda e: e.tensor_copy(r_(ones[:]), tmp[2][:, 0:128]), reads=[c_tmp[2]], writes=[c_ones])
        P.op("dve", lambda e: e.memset(tmp[3][:], 0.0), writes=[c_tmp[3]])
        P.op("dve", lambda e: e.memset(halo[:], 0.0), writes=[c for l in range(NL) for c in c_halo[l]])
        P.op("dve", lambda e: e.memset(schalo[:], 0.0), writes=[c for l in range(NL) for c in c_schalo[l]])
        P.op("dve", lambda e: e.memset(hst[:], 0.0), writes=[c for l in range(NL) for c in c_hst[l]])
        for l in range(NL):
            for h in range(4):
                for dc in range(2):
                    P.op("dve", (lambda l, h, dc: lambda e: e.tensor_copy(r_(rst[l][h][:, dc, :]), tmp[3][:]))(l, h, dc), reads=[c_tmp[3]], writes=[c_rst[l][h]])
        for l in range(NL):
            for j, nm in enumerate(["ffn1_post", "ffn2_post"]):
                o = l * NPC + PO[nm]
                dve_ts(ghalf[:, l, j, :], pp[:, o:o + KC], 0.5, None, ALU.mult, None, [c_pp], [c_der])
            o = l * NPC + PO["lam"]
            t0 = tmp[0][:, 0:KC]
            act(t0, pp[:, o:o + KC], AF.Exp, [c_pp], [c_tmp[0]], scale=-1.0)
            act(t0, t0, AF.Ln, [c_tmp[0]], [c_tmp[0]], bias=1.0)
            dve_ts(nsp[:, l, 0, :], t0, -8.0, None, ALU.mult, None, [c_tmp[0]], [c_der])
            dve_ts(nsp[:, l, 1, :], t0, -16.0, None, ALU.mult, None, [c_tmp[0]], [c_der])

        def norm_stats(src_fn, src_cells):
            b = nb()
            for kc in range(KC):
                t = ntr()
                act(r_(tmpR[t][:]), src_fn(kc), AF.Square, [src_cells[kc]], [c_tmpR[t]])
                P.op("pe", (lambda kc, t: lambda e: e.matmul(banks[b][:], r_(ones[:]), r_(tmpR[t][:]), start=(kc == 0), stop=(kc == KC - 1)))(kc, t),
                     reads=[c_ones, c_tmpR[t]], writes=[c_bank[b]])
            act(rstd[:], banks[b][:], AF.Ln, [c_bank[b]], [c_rstd], scale=1.0 / D, bias=EPS_AP[:])
            act(rstd[:], rstd[:], AF.Exp, [c_rstd], [c_rstd], scale=-0.5)

        def pre_norm(l, gname):
            norm_stats(lambda kc: xres[:, kc, :], c_x)
            for kc in range(KC):
                dve_stt(r_(ubuf[:, kc, :]), xres[:, kc, :], ppcol(l, gname, kc), rstd[:], ALU.mult, ALU.mult,
                        [c_x[kc], c_pp, c_rstd], [c_u[kc]])

        def post_norm_add(gcol_fn):
            norm_stats(lambda kc: ubuf[:, kc, :], c_u)
            for kc in range(KC):
                t = nt()
                dve_stt(tmp[t][:], ubuf[:, kc, :], gcol_fn(kc), rstd[:], ALU.mult, ALU.mult,
                        [c_u[kc], c_pp, c_der, c_rstd], [c_tmp[t]])
                dve_tt(xres[:, kc, :], xres[:, kc, :], tmp[t][:], ALU.add, [c_tmp[t], c_x[kc]], [c_x[kc]])

        def ffn(l, pre, win, wout, ghalf_j, mid_hook=None):
            state["pool"] = POOL_FFN
            pre_norm(l, pre)
            halves = [list(range(0, 12)), list(range(12, 22))]
            for hi, chunks in enumerate(halves):
                base = chunks[0]
                for j0 in range(chunks[0], chunks[-1] + 1, 2):
                    gv, gc = wload(win[l][:, j0 * 128:j0 * 128 + 256], KC, 256)
                    uv, uc = wload(win[l][:, DFF + j0 * 128:DFF + j0 * 128 + 256], KC, 256)
                    for jj in range(2):
                        bg, bu = nb(), nb()
                        gterms = [(r_(gv[:, kc, jj * 128:(jj + 1) * 128]), r_(ubuf[:, kc, :])) for kc in range(KC)]
                        if hi == 0 and j0 == 0 and jj == 0:
                            mmstream(bg, gterms, gc)
                        else:
                            mmgroup(bg, 0, TT, gterms, gc + c_u)
                        mmgroup(bu, 0, TT, [(r_(uv[:, kc, jj * 128:(jj + 1) * 128]), r_(ubuf[:, kc, :])) for kc in range(KC)], uc + c_u)
                        t = nt()
                        act(tmp[t][:], banks[bg][:], AF.Silu, [c_bank[bg]], [c_tmp[t]])
                        j = j0 + jj - base
                        dve_tt(r_(big[:, j, :]), tmp[t][:], banks[bu][:], ALU.mult, [c_tmp[t], c_bank[bu]], [c_big[j]])
                if hi == 0 and mid_hook is not None:
                    mid_hook()
                if hi == 1:
                    act(ssq[:, 7:8], ONE_AP[:], AF.Ln, [c_k], [c_ssq[7]])
                nk = len(chunks)
                ksl = [(0, 8), (8, nk)]
                for npair in range(4):
                    b2 = [nb(), nb()]
                    for si, (k0, k1) in enumerate(ksl):
                        wv, wc = wload(wout[l][(base + k0) * 128:(base + k1) * 128, npair * 256:(npair + 1) * 256], k1 - k0, 256)
                        for nn in range(2):
                            def fn(e, wv=wv, nn=nn, k0=k0, k1=k1, b=b2[nn], si=si):
                                ins = None
                                for k in range(k0, k1):
                                    ins = e.matmul(banks[b][:], r_(wv[:, k - k0, nn * 128:(nn + 1) * 128]), r_(big[:, k, :]),
                                                   start=(si == 0 and k == k0), stop=(si == 1 and k == k1 - 1))
                                return ins
                            P.op("pe", fn, reads=wc + c_big[k0:k1], writes=[c_bank[b2[nn]]])
                    for nn in range(2):
                        n = npair * 2 + nn
                        if hi == 0:
                            act(r_(Sb[:, n, :]), banks[b2[nn]][:], AF.Copy, [c_bank[b2[nn]]], [c_S[n]])
                        else:
                            dve_tt(r_(ubuf[:, n, :]), banks[b2[nn]][:], Sb[:, n, :], ALU.add, [c_bank[b2[nn]], c_S[n]], [c_u[n]])
            post_norm_add(lambda kc: ghalf[:, l, ghalf_j, kc:kc + 1])

        def rope_tables(s):
            P.op("sp", lambda e: e.dma_start(out=posi[:], in_=pos[:, s * TT:(s + 1) * TT].partition_broadcast(128)),
                 writes=[c_posi], key="dpos", inc=16)
            a, b, c = nt(), nt(), nt()
            ang, t1, t2 = tmp[a], tmp[b], tmp[c]
            ca, c1, c2 = c_tmp[a], c_tmp[b], c_tmp[c]
            C1 = 6.28125; C2 = 2 * math.pi - 6.28125
            dve_copy(t1[:], posi[:], [c_posi], [c1])
            dve_ts(ang[:], t1[:], cst[:, C_INV:C_INV + 1], None, ALU.mult, None, [c1, c_cst], [ca])
            dve_ts(t1[:], ang[:], 1.0 / (2 * math.pi), None, ALU.mult, None, [ca], [c1])
            dve_copy(ki[:], t1[:], [c1], [c_ki])
            dve_copy(t1[:], ki[:], [c_ki], [c1])
            dve_stt(t2[:], t1[:], -C1, ang[:], ALU.mult, ALU.add, [c1, ca], [c2])
            dve_stt(ang[:], t1[:], -C2, t2[:], ALU.mult, ALU.add, [c1, c2], [ca])

            def wrap(buf, cb, scratch, cs_):
                dve_ts(scratch[:], buf[:], math.pi, -2 * math.pi, ALU.is_gt, ALU.mult, [cb], [cs_])
                dve_tt(buf[:], buf[:], scratch[:], ALU.add, [cs_, cb], [cb])
                dve_ts(scratch[:], buf[:], -math.pi, 2 * math.pi, ALU.is_lt, ALU.mult, [cb], [cs_])
                dve_tt(buf[:], buf[:], scratch[:], ALU.add, [cs_, cb], [cb])
            wrap(ang, ca, t1, c1)
            act(sinT[:], ang[:], AF.Sin, [ca], [c_cs])
            dve_ts(t2[:], ang[:], math.pi / 2, None, ALU.add, None, [ca], [c2])
            wrap(t2, c2, t1, c1)
            act(cosT[:], t2[:], AF.Sin, [c2], [c_cs])

        def gated_acc(l, gate_off, wname, nkc, first):
            win = W["w_mix_in"]
            for npair in range(4):
                b2 = [nb(), nb()]
                nsl = nkc // 8
                for si in range(nsl):
                    wv, wc = wload(W[wname][l][si * 1024:(si + 1) * 1024, npair * 256:(npair + 1) * 256], 8, 256)
                    for nn in range(2):
                        def fn(e, wv=wv, nn=nn, si=si, b=b2[nn]):
                            ins = None
                            for k in range(8):
                                ins = e.matmul(banks[b][:], r_(wv[:, k, nn * 128:(nn + 1) * 128]), r_(big[:, si * 8 + k, :]),
                                               start=(si == 0 and k == 0), stop=(si == nsl - 1 and k == 7))
                            return ins
                        P.op("pe", fn, reads=wc + c_big[si * 8:(si + 1) * 8], writes=[c_bank[b2[nn]]])
                gv, gc = wload(win[l][:, gate_off + npair * 256:gate_off + (npair + 1) * 256], KC, 256)
                for nn in range(2):
                    n = npair * 2 + nn
                    bg = nb()
                    mmgroup(bg, 0, TT, [(r_(gv[:, kc, nn * 128:(nn + 1) * 128]), r_(ubuf[:, kc, :])) for kc in range(KC)], gc + c_u)
                    t = nt()
                    act(tmp[t][:], banks[bg][:], AF.Sigmoid, [c_bank[bg]], [c_tmp[t]])
                    if first:
                        dve_tt(r_(Sb[:, n, :]), tmp[t][:], banks[b2[nn]][:], ALU.mult, [c_tmp[t], c_bank[b2[nn]]], [c_S[n]])
                    else:
                        dve_tt(tmp[t][:], tmp[t][:], banks[b2[nn]][:], ALU.mult, [c_bank[b2[nn]]], [c_tmp[t]])
                        dve_tt(r_(Sb[:, n, :]), Sb[:, n, :], tmp[t][:], ALU.add, [c_tmp[t]], [c_S[n]])

        def branch_A(l):
            win = W["w_mix_in"]
            xr = [xraw[:, :], rtok[:, 2:4, :].rearrange("p a b -> p (a b)")[:, 0:TT + 4]]
            cxr = [[c_xraw], [c_rtok[2], c_rtok[3]]]
            TB = [[(tmp[i][:], c_tmp[i]) for i in range(4)],
                  [(tmp[4][:], c_tmp[4]), (tmp[5][:], c_tmp[5]), (rtok[:, 0, :], c_rtok[0]), (rtok[:, 1, :], c_rtok[1])]]
            TA = [(rstd[:], c_rstd), (posi[:].bitcast(F32), c_posi)]
            ST = {}

            def S1(p):
                c0 = 2 * p
                xv, xc = wload(win[l][:, c0 * 128:c0 * 128 + 256], KC, 256)
                yv, yc = wload(win[l][:, D + c0 * 128:D + c0 * 128 + 256], KC, 256)
                P.op("pool", (lambda c0=c0: lambda e: e.dma_start(out=r_(wgb[:]), in_=wbd_d[l].rearrange("p (g k n) -> p g k n", g=2, k=KC)[:, :, c0:c0 + 2, :]))(),
                     writes=[c_wgb], key="wg", inc=16)
                bxs, bys, brs, bis = [], [], [], []
                for cc in range(2):
                    bx = nb(); bxs.append(bx)
                    xterms = [(r_(xv[:, kc, cc * 128:(cc + 1) * 128]), r_(ubuf[:, kc, :])) for kc in range(KC)]
                    if p == 0 and cc == 0:
                        mmstream(bx, xterms, xc)
                    else:
                        mmgroup(bx, 0, TT, xterms, xc + c_u)
                for cc in range(2):
                    by = nb(); bys.append(by)
                    mmgroup(by, 0, TT, [(r_(yv[:, kc, cc * 128:(cc + 1) * 128]), r_(ubuf[:, kc, :])) for kc in range(KC)], yc + c_u)
                for cc in range(2):
                    c = c0 + cc
                    ta, cta = TA[cc]
                    X = xr[cc]; cX = cxr[cc]
                    dve_copy(X[:, 0:3], halo[:, l, c, 0:3], [c_halo[l][c]], cX)
                    dve_copy(X[:, 3:3 + TT], banks[bxs[cc]][:], [c_bank[bxs[cc]]], cX)
                    dve_copy(halo[:, l, c, 0:3], X[:, TT:TT + 3], cX, [c_halo[l][c]])
                    cw = lambda k: ppcol(l, "conv_w", c, k)
                    dve_ts(ta, X[:, 3:3 + TT], cw(3), ppcol(l, "conv_b", c), ALU.mult, ALU.add, cX + [c_pp], [cta])
                    dve_stt(ta, X[:, 2:2 + TT], cw(2), ta, ALU.mult, ALU.add, cX + [c_pp], [cta])
                    dve_stt(ta, X[:, 1:1 + TT], cw(1), ta, ALU.mult, ALU.add, cX + [c_pp], [cta])
                    dve_stt(r_(tmpR[cc][:]), X[:, 0:TT], cw(0), ta, ALU.mult, ALU.add, cX + [c_pp, cta], [c_tmpR[cc]])
                    br, bi = nb(), nb(); brs.append(br); bis.append(bi)
                    mmgroup(br, 0, TT, [(r_(wgb[:, 0, cc, :]), r_(tmpR[cc][:]))], [c_wgb, c_tmpR[cc]])
                    mmgroup(bi, 0, TT, [(r_(wgb[:, 1, cc, :]), r_(tmpR[cc][:]))], [c_wgb, c_tmpR[cc]])
                ST[p] = (bys, brs, bis)

            def S2a(p):
                c0 = 2 * p
                bys, brs, bis = ST[p]
                for cc in range(2):
                    c = c0 + cc
                    act(r_(big[:, c, :]), banks[bys[cc]][:], AF.Gelu_apprx_tanh, [c_bank[bys[cc]]], [c_big[c]])
                for cc in range(2):
                    c = c0 + cc
                    (tr, ctr), (ti, cti), (tA, ctA), (tm, ctm) = TB[cc]
                    act(tr, banks[brs[cc]][:], AF.Sigmoid, [c_bank[brs[cc]], c_pp], [ctr], bias=ppcol(l, "b_a", c))
                    act(ti, banks[bis[cc]][:], AF.Sigmoid, [c_bank[bis[cc]], c_pp], [cti], bias=ppcol(l, "b_x", c))
                for cc in range(2):
                    c = c0 + cc
                    (tr, ctr), (ti, cti), (tA, ctA), (tm, ctm) = TB[cc]
                    act(tA, tr, AF.Exp, [ctr, c_der], [ctA], scale=nsp[:, l, 0, c:c + 1])
                for cc in range(2):
                    (tr, ctr), (ti, cti), (tA, ctA), (tm, ctm) = TB[cc]
                    dve_tt(ti, ti, tmpR[cc][:], ALU.mult, [c_tmpR[cc]], [cti])
                    dve_tt(tm, tA, tA, ALU.mult, [ctA], [ctm])

            def S2b(p):
                c0 = 2 * p
                for cc in range(2):
                    (tr, ctr), (ti, cti), (tA, ctA), (tm, ctm) = TB[cc]
                    act(tm, tm, AF.Sqrt, [ctm], [ctm], scale=-1.0, bias=ONE_AP[:])
                for cc in range(2):
                    c = c0 + cc
                    (tr, ctr), (ti, cti), (tA, ctA), (tm, ctm) = TB[cc]
                    dve_tt(ti, ti, tm, ALU.mult, [ctm], [cti])
                    P.op("dve", (lambda tr=tr, tA=tA, ti=ti, c=c: lambda e: e.tensor_tensor_scan(tr, tA, ti, hst[:, l, c:c + 1], ALU.mult, ALU.add))(),
                         reads=[ctA, cti, c_hst[l][c]], writes=[ctr])
                    dve_copy(hst[:, l, c:c + 1], tr[:, TT - 1:TT], [ctr], [c_hst[l][c]])
                    dve_tt(r_(big[:, c, :]), tr, big[:, c, :], ALU.mult, [ctr], [c_big[c]])

            S1(0)
            for p in range(4):
                S2a(p)
                if p < 3:
                    S1(p + 1)
                S2b(p)

        def branch_C(l):
            win = W["w_mix_in"]
            for c0 in range(0, KC, 2):
                bv, bc_ = wload(win[l][:, 8192 + c0 * 128:8192 + c0 * 128 + 256], KC, 256)
                cv, cc_ = wload(win[l][:, 9216 + c0 * 128:9216 + c0 * 128 + 256], KC, 256)
                hv, hc_ = wload(win[l][:, 10240 + c0 * 128:10240 + c0 * 128 + 256], KC, 256)
                for cc in range(2):
                    c = c0 + cc
                    b_b, b_c, b_h = nb(), nb(), nb()
                    for (b, v, vc) in [(b_c, cv, cc_), (b_h, hv, hc_), (b_b, bv, bc_)]:
                        mmgroup(b, 0, TT, [(r_(v[:, kc, cc * 128:(cc + 1) * 128]), r_(ubuf[:, kc, :])) for kc in range(KC)], vc + c_u)
                    t1, t2, t3 = nt(), nt(), nt()
                    act(tmp[t1][:], banks[b_c][:], AF.Copy, [c_bank[b_c]], [c_tmp[t1]])
                    act(tmp[t3][:], banks[b_b][:], AF.Copy, [c_bank[b_b]], [c_tmp[t3]])
                    dve_copy(xraw[:, 0:2], schalo[:, l, c, 0:2], [c_schalo[l][c]], [c_xraw])
                    dve_tt(xraw[:, 2:2 + TT], tmp[t1][:], banks[b_h][:], ALU.mult, [c_tmp[t1], c_bank[b_h]], [c_xraw])
                    dve_copy(schalo[:, l, c, 0:2], xraw[:, TT:TT + 2], [c_xraw], [c_schalo[l][c]])
                    sw = lambda k: ppcol(l, "sc_w", c, k)
                    dve_ts(tmp[t2][:], xraw[:, 2:2 + TT], sw(2), None, ALU.mult, None, [c_xraw, c_pp], [c_tmp[t2]])
                    dve_stt(tmp[t2][:], xraw[:, 1:1 + TT], sw(1), tmp[t2][:], ALU.mult, ALU.add, [c_xraw, c_pp], [c_tmp[t2]])
                    dve_stt(tmp[t2][:], xraw[:, 0:TT], sw(0), tmp[t2][:], ALU.mult, ALU.add, [c_xraw, c_pp], [c_tmp[t2]])
                    dve_tt(r_(big[:, c, :]), tmp[t2][:], tmp[t3][:], ALU.mult, [c_tmp[t2], c_tmp[t3]], [c_big[c]])

        def branch_B(l):
            win = W["w_mix_in"]
            ident = cst[:, C_ID:C_ID + 128]
            OFF = [0, 512, 896, 1152]
            ktv = kT[:].rearrange("p a b -> p (a b)").rearrange("p (b d) -> p b d", b=4)
            HS = {}

            def P1(h):
                GP = G64[h]
                for (dst, cdst, off) in [(qT, c_qT, 2048 + h * 256), (kT, c_kT, 3072 + h * 256)]:
                    wv, wc = wload(win[l][:, off:off + 256], KC, 256)
                    b0, b1 = nb(), nb()
                    for dc, b in enumerate([b0, b1]):
                        mmgroup(b, 0, TT, [(r_(wv[:, kc, dc * 128:(dc + 1) * 128]), r_(ubuf[:, kc, :])) for kc in range(KC)], wc + c_u)
                    t1, t2 = nt(), nt()
                    dve_tt(tmp[t1][:], banks[b0][:], cosT[:], ALU.mult, [c_bank[b0], c_cs], [c_tmp[t1]])
                    dve_tt(tmp[t2][:], banks[b1][:], sinT[:], ALU.mult, [c_bank[b1], c_cs], [c_tmp[t2]])
                    dve_tt(r_(dst[:, 0, :]), tmp[t1][:], tmp[t2][:], ALU.subtract, [c_tmp[t1], c_tmp[t2]], [cdst])
                    dve_tt(tmp[t1][:], banks[b0][:], sinT[:], ALU.mult, [c_bank[b0], c_cs], [c_tmp[t1]])
                    dve_tt(tmp[t2][:], banks[b1][:], cosT[:], ALU.mult, [c_bank[b1], c_cs], [c_tmp[t2]])
                    dve_tt(r_(dst[:, 1, :]), tmp[t1][:], tmp[t2][:], ALU.add, [c_tmp[t1], c_tmp[t2]], [cdst])

            def P2(h):
                GP = G64[h]
                bsc = []
                for jb in range(4):
                    bs = nb(); bsc.append(bs)

                    def fn_sc(e, bs=bs, jb=jb):
                        ins = None
                        for dc in range(2):
                            ins = e.matmul(banks[bs][:, jb * 128:TT], r_(kT[:, dc, jb * 128:(jb + 1) * 128]),
                                           r_(qT[:, dc, jb * 128:TT]), start=(dc == 0), stop=(dc == 1))
                        return ins
                    P.op("pe", fn_sc, reads=[c_qT, c_kT], writes=[c_bank[bs]])
                bts = []
                for half in range(2):
                    bt = nb(); bts.append(bt)

                    def fn_t(e, bt=bt, half=half):
                        ins = None
                        for bb in range(2):
                            blk = half * 2 + bb
                            for dc in range(2):
                                ins = e.transpose(banks[bt][:, bb * 256 + dc * 128:bb * 256 + (dc + 1) * 128],
                                                  kT[:, dc, blk * 128:(blk + 1) * 128], ident)
                        return ins
                    P.op("pe", fn_t, reads=[c_kT, c_cst], writes=[c_bank[bt]])
                mkD = cst[:, C_MASK + h * 128:C_MASK + (h + 1) * 128]
                mkB = cst[:, C_BASE + h * 128:C_BASE + (h + 1) * 128]
                for jb in range(4):
                    bs = bsc[jb]
                    dve_tt(r_(S2[:, OFF[jb]:OFF[jb] + 128]), banks[bs][:, jb * 128:(jb + 1) * 128], mkD, ALU.mult, [c_bank[bs], c_cst], [c_S2])
                    for ib in range(jb + 1, 4):
                        d = ib - jb
                        dve_stt(r_(S2[:, OFF[jb] + d * 128:OFF[jb] + (d + 1) * 128]), banks[bs][:, ib * 128:(ib + 1) * 128], GP[d], mkB,
                                ALU.mult, ALU.mult, [c_bank[bs], c_cst], [c_S2])
                for jb in range(4):
                    half, bb = jb // 2, jb % 2
                    act(r_(ktv[:, jb, :]), banks[bts[half]][:, bb * 256:(bb + 1) * 256], AF.Copy, [c_bank[bts[half]], c_cst], [c_kT],
                        scale=cst[:, C_KD + h * 4 + jb:C_KD + h * 4 + jb + 1])
                bvs = [nb() for _ in range(4)]
                for hf in range(2):
                    wv, wc = wload(win[l][:, 4096 + h * 512 + hf * 256:4096 + h * 512 + (hf + 1) * 256], KC, 256)
                    for blk in range(4):
                        mmgroup(bvs[blk], hf * 256, 256, [(r_(ubuf[:, kc, blk * 128:(blk + 1) * 128]), r_(wv[:, kc, :])) for kc in range(KC)], wc + c_u)
                for blk in range(4):
                    act(r_(vtok[:, blk, :]), banks[bvs[blk]][:], AF.Copy, [c_bank[bvs[blk]]], [c_vtok[blk]])
                qtab = cst[:, C_QD + h * 128:C_QD + (h + 1) * 128]
                for ib in range(4):
                    P.op("dve", (lambda ib=ib, qtab=qtab, GP=GP: lambda e: e.scalar_tensor_tensor(
                        r_(qd[:, :, ib * 128:(ib + 1) * 128]), qT[:, :, ib * 128:(ib + 1) * 128], GP[ib],
                        qtab.unsqueeze(1).to_broadcast([128, 2, 128]), ALU.mult, ALU.mult))(), reads=[c_qT, c_cst], writes=[c_qd])
                S0 = rst[l][h]; cS0 = c_rst[l][h]
                bq = [nb(), nb()]
                for dc in range(2):
                    mmgroup(bq[dc], 0, TT, [(r_(ktv[:, jb, dc * 128:(dc + 1) * 128]), r_(vtok[:, jb, :])) for jb in range(4)], [c_kT] + c_vtok)
                HS[h] = (bq,)
                HELD.update(bq)

            def P3(h):
                GP = G64[h]
                S0 = rst[l][h]; cS0 = c_rst[l][h]
                (bq,) = HS[h]
                sb0 = (h % 2) * 4
                bos = []
                for ib in range(4):
                    bo = nb(); bos.append(bo)
                    terms = [(r_(S2[:, OFF[jb] + (ib - jb) * 128:OFF[jb] + (ib - jb + 1) * 128]), r_(vtok[:, jb, :])) for jb in range(ib + 1)]
                    terms += [(r_(qd[:, dc, ib * 128:(ib + 1) * 128]), r_(S0[:, dc, :])) for dc in range(2)]
                    mmgroup(bo, 0, TT, terms, [c_S2, c_qd, cS0] + c_vtok[0:ib + 1])
                    t = nt()
                    act(tmp[t][:], banks[bo][:], AF.Square, [c_bank[bo]], [c_tmp[t], c_ssq[sb0 + ib]], accum_out=ssq[:, sb0 + ib:sb0 + ib + 1])
                act(ssq[:, sb0:sb0 + 4], ssq[:, sb0:sb0 + 4], AF.Ln, c_ssq[sb0:sb0 + 4], c_ssq[sb0:sb0 + 4], scale=1.0 / 512, bias=EPS_AP[:])
                act(ssq[:, sb0:sb0 + 4], ssq[:, sb0:sb0 + 4], AF.Exp, c_ssq[sb0:sb0 + 4], c_ssq[sb0:sb0 + 4], scale=-0.5)
                for ib in range(4):
                    act(rtok[:, ib, :], banks[bos[ib]][:], AF.Copy, [c_bank[bos[ib]], c_ssq[sb0 + ib]], [c_rtok[ib]], scale=ssq[:, sb0 + ib:sb0 + ib + 1])
                for dc in range(2):
                    dve_stt(r_(S0[:, dc, :]), S0[:, dc, :], GP[4], banks[bq[dc]][:], ALU.mult, ALU.add, [cS0, c_bank[bq[dc]]], [cS0])
                HELD.difference_update(bq)
                GS = {}

                def Gp(ec):
                    gh, e2 = ec // 2, ec % 2
                    if e2 == 0:
                        GS["w"] = wload(win[l][:, 6144 + h * 512 + gh * 256:6144 + h * 512 + (gh + 1) * 256], KC, 256)
                    wv, wc = GS["w"]
                    bg = nb()
                    mmgroup(bg, 0, TT, [(r_(wv[:, kc, e2 * 128:(e2 + 1) * 128]), r_(ubuf[:, kc, :])) for kc in range(KC)], wc + c_u)
                    t = nt()
                    act(tmp[t][:], banks[bg][:], AF.Silu, [c_bank[bg]], [c_tmp[t]])
                    GS[ec] = t

                def Tp(ec):
                    t = GS[ec]
                    bt = nb()

                    def fn_t2(e, bt=bt, ec=ec):
                        ins = None
                        for blk in range(4):
                            ins = e.transpose(banks[bt][:, blk * 128:(blk + 1) * 128], rtok[:, blk, ec * 128:(ec + 1) * 128], ident)
                        return ins
                    P.op("pe", fn_t2, reads=c_rtok + [c_cst], writes=[c_bank[bt]])
                    dve_tt(r_(big[:, h * 4 + ec, :]), tmp[t][:], banks[bt][:], ALU.mult, [c_tmp[t], c_bank[bt]], [c_big[h * 4 + ec]])
                Gp(0); Gp(1); Tp(0); Gp(2); Tp(1); Gp(3); Tp(2); Tp(3)

            P1(0)
            for h in range(4):
                P2(h)
                if h < 3:
                    P1(h + 1)
                P3(h)

        def dump_big(s, n):
            P.op("sp", lambda e: e.dma_start(out=dbg[s][:, 0:n, :], in_=big[:, 0:n, :]), reads=c_big[0:n], key="ddbg", inc=16)

        def dump_S(s):
            P.op("sp", lambda e: e.dma_start(out=dbg[s][:, 0:KC, :], in_=Sb[:]), reads=c_S, key="ddbg", inc=16)

        def mixer(l, s):
            state["pool"] = POOL_MIX
            pre_norm(l, "mix_pre")
            if dump == "U":
                P.op("sp", lambda e: e.dma_start(out=dbg[s][:, 0:KC, :], in_=ubuf[:]), reads=c_u, key="ddbg", inc=16)
                return False
            if dump in (None, "A", "S"):
                branch_A(l)
                if dump == "A":
                    dump_big(s, 8); return False
                gated_acc(l, 11264, "w_lru_out", 8, True)
            if dump in (None, "B", "S"):
                state["pool"] = POOL_B
                branch_B(l)
                state["pool"] = POOL_MIX
                if dump == "B":
                    dump_big(s, 16); return False
                gated_acc(l, 12288, "w_ret_out", 16, False)
            if dump in (None, "C", "S"):
                state["pool"] = POOL_C
                branch_C(l)
                if dump == "C":
                    dump_big(s, 8); return False
                gated_acc(l, 13312, "w_sc_out", 8, False)
            if dump == "S":
                dump_S(s); return False
            act(ssq[:, 7:8], ONE_AP[:], AF.Ln, [c_k], [c_ssq[7]])
            for npair in range(4):
                wv, wc = wload(W["w_mix_out"][l][:, npair * 256:(npair + 1) * 256], KC, 256)
                for nn in range(2):
                    n = npair * 2 + nn
                    b = nb()
                    mmgroup(b, 0, TT, [(r_(wv[:, kc, nn * 128:(nn + 1) * 128]), r_(Sb[:, kc, :])) for kc in range(KC)], wc + c_S)
                    act(r_(ubuf[:, n, :]), banks[b][:], AF.Copy, [c_bank[b]], [c_u[n]])
            post_norm_add(lambda kc: ppcol(l, "mix_post", kc))
            return True

        EPS_AP = sb("eps_ap", [128, 1]); ONE_AP = sb("one_ap", [128, 1]); c_k = Cell("k")
        P.op("dve", lambda e: e.memset(EPS_AP[:], EPS), writes=[c_k])
        P.op("dve", lambda e: e.memset(ONE_AP[:], 1.0), writes=[c_k])
        P.op("act", lambda e: e.activation(tmp[1][:, 0:1], EPS_AP[:], AF.Copy), reads=[c_k], writes=[c_tmp[1]])

        out_toks = []
        for s in range(NT):
            P.op("sp", (lambda s: lambda e: e.dma_start(out=xres[:], in_=xTv[:, :, s * TT:(s + 1) * TT]))(s), writes=c_x, key="dxin", inc=16)
            go = True
            for l in range(n_layers):
                if not go:
                    break
                ffn(l, "ffn1_pre", W["ffn1_w_in"], W["ffn1_w_out"], 0, mid_hook=((lambda s=s: rope_tables(s)) if l == 0 else None))
                if stop == (l, "ffn1"):
                    break
                go = mixer(l, s)
                if not go or stop == (l, "mix"):
                    break
                ffn(l, "ffn2_pre", W["ffn2_w_in"], W["ffn2_w_out"], 1)
                if stop == (l, "ffn2"):
                    break
            out_toks.append(P.op("sp", (lambda s: lambda e: e.dma_start(out=oTv[:, :, s * TT:(s + 1) * TT], in_=xres[:]))(s), reads=c_x, key="dxout", inc=16))
        fin = [out_toks[-1]]
        if dump:
            fin.append(("ddbg", P.cnt["ddbg"]))
        P.wait_all("sp", fin)

        sems = {}
        for k in sorted(P.cnt.keys()):
            sems[k] = st.enter_context(nc.semaphore("s_" + k))
        block = st.enter_context(nc.Block())

        @block.tensor
        def _(e):
            P.replay("pe", e, sems)

        @block.scalar
        def _(e):
            P.replay("act", e, sems)

        @block.vector
        def _(e):
            P.replay("dve", e, sems)

        @block.gpsimd
        def _(e):
            P.replay("pool", e, sems)

        @block.sync
        def _(e):
            P.replay("sp", e, sems)
    return nc


def pack_inputs(inputs):
    L = NL

    def pk(v):
        return np.ascontiguousarray(np.asarray(v, np.float32).reshape(KC, 128).T)
    pp = np.zeros((128, L * NPC), np.float32)
    for l in range(L):
        o = l * NPC
        for nm, key in [("ffn1_pre", "ffn1_pre_g"), ("ffn1_post", "ffn1_post_g"), ("mix_pre", "mix_pre_g"), ("mix_post", "mix_post_g"),
                        ("ffn2_pre", "ffn2_pre_g"), ("ffn2_post", "ffn2_post_g"), ("conv_b", "lru_conv_b"), ("b_a", "lru_b_a"),
                        ("b_x", "lru_b_x"), ("lam", "lru_lambda")]:
            pp[:, o + PO[nm]:o + PO[nm] + KC] = pk(inputs[key][l])
        for k in range(4):
            pp[:, o + PO["conv_w"] + k * KC:o + PO["conv_w"] + (k + 1) * KC] = pk(inputs["lru_conv_w"][l][k])
        for k in range(3):
            pp[:, o + PO["sc_w"] + k * KC:o + PO["sc_w"] + (k + 1) * KC] = pk(inputs["sc_conv_w"][l][k])
    wbd = np.zeros((L, 128, 2, KC, 128), np.float32)
    for l in range(L):
        for g, key in enumerate(["lru_w_a", "lru_w_x"]):
            w = np.asarray(inputs[key][l], np.float32)
            for c in range(KC):
                wbd[l, 0:64, g, c, 0:64] = w[2 * c]
                wbd[l, 64:128, g, c, 64:128] = w[2 * c + 1]
    wbd = wbd.reshape(L, 128, 2 * KC * 128)
    cst, _ = head_consts()
    common = {"pos": np.asarray(inputs["positions"], np.int32).reshape(1, SEQ), "pp": pp, "cst": cst, "wbd": wbd}
    for nm in ["ffn1_w_in", "ffn1_w_out", "w_mix_in", "w_lru_out", "w_ret_out", "w_sc_out", "w_mix_out", "ffn2_w_in", "ffn2_w_out"]:
        common[nm] = np.ascontiguousarray(np.asarray(inputs[nm], np.float32))
    return common


_NC_CACHE = {}


def kernel(**inputs):
    x = np.asarray(inputs["x"], np.float32)
    B = x.shape[0]
    common = pack_inputs(inputs)
    in_maps = []
    for b in range(B):
        m = dict(common)
        m["xT"] = np.ascontiguousarray(x[b].T)
        in_maps.append(m)
    if "nc" not in _NC_CACHE:
        _NC_CACHE["nc"] = build()
    res = run_bass_kernel_spmd(_NC_CACHE["nc"], in_maps, core_ids=list(range(B)))
    out = np.stack([np.ascontiguousarray(r["outT"].T) for r in res.results], 0)
    return out.astype(np.float32)
```
